# Optimizing a Trainium2 kernel written in Bass

```python
import jax, jax.numpy as jnp
from jax import lax
import numpy as np

D_MODEL = 2048
BATCH = 1
SEQ = 16384
DEPTH = 4

N_MIXERS = 3
N_POOL_LAYERS = (DEPTH + 2) // 3
N_SGU_LAYERS = (DEPTH + 1) // 3
N_ATTN_LAYERS = DEPTH // 3
RMS_EPS = 1e-6
D_FF = 4 * D_MODEL

POOL_WINDOWS = (2, 4, 8, 16)
POOL_N_GROUPS = len(POOL_WINDOWS)
POOL_GROUP_CH = D_MODEL // POOL_N_GROUPS

SGU_WIDTH = D_MODEL
SGU_CHUNK = 128
SGU_GROUPS = 8
SGU_GROUP_CH = SGU_WIDTH // SGU_GROUPS

ATTN_PATTERNS = ((128, 1), (512, 4), (2048, 16))
ATTN_GROUPS = len(ATTN_PATTERNS)
ATTN_HEADS = 8
HEAD_DIM = 128
ATTN_BLOCK = 128
ATTN_OUT_WIDTH = ATTN_HEADS * HEAD_DIM
ATTN_QKV_WIDTH = 3 * ATTN_GROUPS * ATTN_HEADS * HEAD_DIM
NEG_INF = -1e30

kernel_name = "hybrid_pool_sgu_dilated_attn_trunk"


def rmsnorm(x, gain):
    xf = x.astype(jnp.float32)
    y = xf * lax.rsqrt(jnp.mean(xf * xf, axis=-1, keepdims=True) + RMS_EPS)
    return (y * gain.astype(jnp.float32)).astype(x.dtype)


def pool_mixer(x, w_in, w_group, scale, w_out):
    B, S, _ = x.shape
    h = (x @ w_in).astype(jnp.float32).reshape(B, S, POOL_N_GROUPS, POOL_GROUP_CH)
    cs = jnp.cumsum(h, axis=1)
    pos = jnp.arange(S)
    outs = []
    for g, w in enumerate(POOL_WINDOWS):
        c = cs[:, :, g]
        c_prev = jnp.pad(c, ((0, 0), (w, 0), (0, 0)))[:, :S]
        count = jnp.minimum(pos + 1, w).astype(jnp.float32)[None, :, None]
        outs.append((c - c_prev) / count - h[:, :, g])
    pooled = jnp.stack(outs, axis=2)
    mixed = jnp.einsum('bsgc,gcd->bsgd', pooled, w_group.astype(jnp.float32))
    mixed = mixed.reshape(B, S, D_MODEL) * scale.astype(jnp.float32)
    return mixed.astype(x.dtype) @ w_out


def sgu_mixer(x, w_in, v_norm, w_s, b_s, w_out):
    B, S, _ = x.shape
    h = jax.nn.gelu(x @ w_in, approximate=False)
    u, v = jnp.split(h, 2, axis=-1)
    v = rmsnorm(v, v_norm)
    n_chunks = S // SGU_CHUNK
    vc = v.reshape(B, n_chunks, SGU_CHUNK, SGU_GROUPS, SGU_GROUP_CH)
    causal = jnp.tril(jnp.ones((SGU_CHUNK, SGU_CHUNK), dtype=bool))
    ws = jnp.where(causal[None], w_s, jnp.zeros_like(w_s))
    sp = jnp.einsum('gts,bnsgc->bntgc', ws, vc) + b_s.T[None, None, :, :, None]
    gated = u * sp.reshape(B, S, SGU_WIDTH)
    return gated @ w_out


def dilated_group_attention(q, k, v, window, dilation):
    B, S, H, Dh = q.shape
    n_keys = window // dilation + 1
    scale = HEAD_DIM ** -0.5
    k_pad = jnp.pad(k, ((0, 0), (window, 0), (0, 0), (0, 0)))
    v_pad = jnp.pad(v, ((0, 0), (window, 0), (0, 0), (0, 0)))
    qi = jnp.arange(ATTN_BLOCK)
    kj = jnp.arange(n_keys)
    offsets = qi[:, None] - dilation * kj[None, :]
    local_idx = window + offsets

    def block(blk):
        s0 = blk * ATTN_BLOCK
        qb = lax.dynamic_slice_in_dim(q, s0, ATTN_BLOCK, axis=1).astype(jnp.float32)
        kw = lax.dynamic_slice_in_dim(k_pad, s0, window + ATTN_BLOCK, axis=1)
        vw = lax.dynamic_slice_in_dim(v_pad, s0, window + ATTN_BLOCK, axis=1)
        kg = kw[:, local_idx].astype(jnp.float32)
        vg = vw[:, local_idx].astype(jnp.float32)
        s = jnp.einsum('bqhd,bqjhd->bhqj', qb, kg) * scale
        valid = (s0 + offsets) >= 0
        s = jnp.where(valid[None, None], s, jnp.float32(NEG_INF))
        m = jnp.max(s, axis=-1, keepdims=True)
        p = jnp.exp(s - m)
        den = jnp.sum(p, axis=-1, keepdims=True)
        o = jnp.einsum('bhqj,bqjhd->bqhd', p, vg)
        o = o / jnp.moveaxis(den[..., 0], 1, 2)[..., None]
        lse = jnp.moveaxis((m + jnp.log(den))[..., 0], 1, 2)
        return o, lse

    outs, lses = lax.map(block, jnp.arange(S // ATTN_BLOCK))
    o = jnp.moveaxis(outs, 0, 1).reshape(B, S, H, Dh)
    lse = jnp.moveaxis(lses, 0, 1).reshape(B, S, H)
    return o, lse


def attn_mixer(x, w_qkv, w_out):
    B, S, _ = x.shape
    qkv = (x @ w_qkv).reshape(B, S, 3, ATTN_GROUPS, ATTN_HEADS, HEAD_DIM)
    outs, lses = [], []
    for g, (window, dilation) in enumerate(ATTN_PATTERNS):
        o, lse = dilated_group_attention(qkv[:, :, 0, g], qkv[:, :, 1, g], qkv[:, :, 2, g],
                                         window, dilation)
        outs.append(o)
        lses.append(lse)
    weights = jax.nn.softmax(jnp.stack(lses, axis=0), axis=0)
    o = jnp.sum(weights[..., None] * jnp.stack(outs, axis=0), axis=0)
    return o.reshape(B, S, ATTN_OUT_WIDTH).astype(x.dtype) @ w_out


def squared_relu_mlp(x, w_up, w_down):
    h = jax.nn.relu(x @ w_up)
    return (h * h) @ w_down


def setup_inputs(seed: int = 0) -> dict:
    key = jax.random.key(seed)
    ks = jax.random.split(key, 20)
    f32 = jnp.float32

    def nrm(k, shape, fan_in):
        return jax.random.normal(k, shape, f32) * (fan_in ** -0.5)

    def gain(k, shape):
        return 1.0 + 0.05 * jax.random.normal(k, shape, f32)

    x = jax.random.normal(ks[0], (BATCH, SEQ, D_MODEL), f32)
    norm_mix = gain(ks[1], (DEPTH, D_MODEL))
    pool_w_in = nrm(ks[2], (N_POOL_LAYERS, D_MODEL, D_MODEL), D_MODEL)
    pool_w_group = nrm(ks[3], (N_POOL_LAYERS, POOL_N_GROUPS, POOL_GROUP_CH, POOL_GROUP_CH), POOL_GROUP_CH)
    pool_scale = gain(ks[4], (N_POOL_LAYERS, D_MODEL))
    pool_w_out = nrm(ks[5], (N_POOL_LAYERS, D_MODEL, D_MODEL), D_MODEL)
    sgu_w_in = nrm(ks[6], (N_SGU_LAYERS, D_MODEL, 2 * SGU_WIDTH), D_MODEL)
    sgu_v_norm = gain(ks[7], (N_SGU_LAYERS, SGU_WIDTH))
    sgu_w_s = nrm(ks[8], (N_SGU_LAYERS, SGU_GROUPS, SGU_CHUNK, SGU_CHUNK), SGU_CHUNK)
    sgu_b_s = 1.0 + 0.1 * jax.random.normal(ks[9], (N_SGU_LAYERS, SGU_GROUPS, SGU_CHUNK), f32)
    sgu_w_out = nrm(ks[10], (N_SGU_LAYERS, SGU_WIDTH, D_MODEL), SGU_WIDTH)
    attn_w_qkv = nrm(ks[11], (N_ATTN_LAYERS, D_MODEL, ATTN_QKV_WIDTH), D_MODEL)
    attn_w_out = nrm(ks[12], (N_ATTN_LAYERS, ATTN_OUT_WIDTH, D_MODEL), ATTN_OUT_WIDTH)
    norm_mlp = gain(ks[13], (DEPTH, D_MODEL))
    mlp_w_up = nrm(ks[14], (DEPTH, D_MODEL, D_FF), D_MODEL)
    mlp_w_down = nrm(ks[15], (DEPTH, D_FF, D_MODEL), D_FF)
    norm_final = gain(ks[16], (D_MODEL,))
    return {"x": x, "norm_mix": norm_mix,
            "pool_w_in": pool_w_in, "pool_w_group": pool_w_group,
            "pool_scale": pool_scale, "pool_w_out": pool_w_out,
            "sgu_w_in": sgu_w_in, "sgu_v_norm": sgu_v_norm, "sgu_w_s": sgu_w_s,
            "sgu_b_s": sgu_b_s, "sgu_w_out": sgu_w_out,
            "attn_w_qkv": attn_w_qkv, "attn_w_out": attn_w_out,
            "norm_mlp": norm_mlp, "mlp_w_up": mlp_w_up, "mlp_w_down": mlp_w_down,
            "norm_final": norm_final}


def reference(x, norm_mix, pool_w_in, pool_w_group, pool_scale, pool_w_out,
              sgu_w_in, sgu_v_norm, sgu_w_s, sgu_b_s, sgu_w_out,
              attn_w_qkv, attn_w_out, norm_mlp, mlp_w_up, mlp_w_down, norm_final):
    for i in range(DEPTH):
        kind = i % N_MIXERS
        j = i // N_MIXERS
        h = rmsnorm(x, norm_mix[i])
        if kind == 0:
            y = pool_mixer(h, pool_w_in[j], pool_w_group[j], pool_scale[j], pool_w_out[j])
        elif kind == 1:
            y = sgu_mixer(h, sgu_w_in[j], sgu_v_norm[j], sgu_w_s[j], sgu_b_s[j], sgu_w_out[j])
        else:
            y = attn_mixer(h, attn_w_qkv[j], attn_w_out[j])
        x = x + y.astype(x.dtype)
        h = rmsnorm(x, norm_mlp[i])
        x = x + squared_relu_mlp(h, mlp_w_up[i], mlp_w_down[i]).astype(x.dtype)
    return rmsnorm(x, norm_final)
```

```python
import contextlib
import numpy as np
import ml_dtypes
import concourse.bass as bass
import concourse.mybir as mybir
from concourse.bass_utils import run_bass_kernel_spmd

F32 = mybir.dt.float32
BF16 = mybir.dt.bfloat16
AF = mybir.ActivationFunctionType
ALU = mybir.AluOpType
AX = mybir.AxisListType

ENGS = ["pe", "act", "dve", "pool", "sp"]


class Sched:
    def __init__(self, nc):
        self.nc = nc
        self.ops = []
        self.eng_ops = {e: [] for e in ENGS}
        self.lastw = {}
        self.readers = {}
        self.dma_count = {}
        self.last_dma = {}
        self.stack = contextlib.ExitStack()

    def sb(self, name, shape, dt):
        return self.stack.enter_context(self.nc.sbuf_tensor(name, shape, dt))

    def ps(self, name, shape, dt):
        return self.stack.enter_context(self.nc.psum_tensor(name, shape, dt))

    def add(self, eng, fn, reads=(), writes=(), dma_key=None):
        oid = len(self.ops)
        deps = set()
        rk = list(reads)
        wk = list(writes)
        if dma_key is not None:
            wk.append(("dmaq", dma_key))
        for k in rk:
            if k in self.lastw:
                deps.add(self.lastw[k])
        for k in wk:
            if k in self.lastw:
                deps.add(self.lastw[k])
            for r in self.readers.get(k, ()):
                deps.add(r)
        for k in rk:
            self.readers.setdefault(k, []).append(oid)
        for k in wk:
            self.lastw[k] = oid
            self.readers[k] = []
        op = dict(id=oid, eng=eng, fn=fn, deps=deps, dma_key=dma_key, ms=None, need=False)
        if dma_key is not None:
            c = self.dma_count.get(dma_key, 0) + 1
            self.dma_count[dma_key] = c
            op["dma_cnt"] = c
            self.last_dma[dma_key] = oid
        self.ops.append(op)
        self.eng_ops[eng].append(op)
        return oid

    def barrier(self):
        last = set()
        for e in ENGS:
            for op in reversed(self.eng_ops[e]):
                if op["fn"] is not None and op["dma_key"] is None:
                    last.add(op["id"])
                    break
        for k, oid in self.last_dma.items():
            last.add(oid)
        for e in ENGS:
            oid = len(self.ops)
            op = dict(id=oid, eng=e, fn=None, deps=set(last), dma_key=None, ms=None, need=False)
            self.ops.append(op)
            self.eng_ops[e].append(op)
        self.lastw = {}
        self.readers = {}

    def emit(self, final_wait=True):
        nc = self.nc
        ops = self.ops
        if final_wait:
            self.barrier()
        for op in ops:
            for d in op["deps"]:
                dop = ops[d]
                if dop["dma_key"] is None:
                    dop["need"] = True
        cnt = {e: 0 for e in ENGS}
        for e in ENGS:
            for op in self.eng_ops[e]:
                if op["need"]:
                    cnt[e] += 1
                    op["ms"] = cnt[e]
        st = self.stack
        sem_e = {e: st.enter_context(nc.semaphore("se_" + e)) for e in ENGS}
        sem_d = {}
        for i, k in enumerate(self.dma_count):
            sem_d[k] = st.enter_context(nc.semaphore("sd_%d" % i))
        self.n_sems = len(sem_e) + len(sem_d)
        self.n_waits = 0
        block = st.enter_context(nc.Block())

        def body(ename):
            def f(eng):
                waited = {}
                for op in self.eng_ops[ename]:
                    for d in sorted(op["deps"]):
                        dop = ops[d]
                        if dop["dma_key"] is not None:
                            sem, val = sem_d[dop["dma_key"]], 16 * dop["dma_cnt"]
                        else:
                            if dop["eng"] == ename and ename == "pe":
                                continue
                            sem, val = sem_e[dop["eng"]], dop["ms"]
                        key = id(sem)
                        if waited.get(key, 0) >= val:
                            continue
                        waited[key] = val
                        eng.wait_ge(sem, val)
                        self.n_waits += 1
                    if op["fn"] is None:
                        continue
                    ins = op["fn"](eng)
                    if op["dma_key"] is not None:
                        ins.then_inc(sem_d[op["dma_key"]], 16)
                    elif op["ms"] is not None:
                        ins.then_inc(sem_e[ename], 1)
            return f

        block.tensor(body("pe"))
        block.scalar(body("act"))
        block.vector(body("dve"))
        block.gpsimd(body("pool"))
        block.sync(body("sp"))
        st.close()


NCORES = 8
D = 2048
KC = 16
OWN = 2048
NB = 34
NT = NB * 128
HALO = NT - OWN
TM = 896
EPS = 1e-6
SCALE = 128.0 ** -0.5
GW = (128, 512, 2048)
GD = (1, 4, 16)
SC_ORDER = [(2, k) for k in range(17)] + [(0, 0), (0, 1), (1, 4)] + [(1, k) for k in range(4)]
NKB = len(SC_ORDER)
NSC = NKB * 128


def tiles_of(b0, b1, first):
    out = [(b0, first)] if first else []
    b = b0 + first
    while b < b1:
        n = min(7, b1 - b)
        out.append((b, n))
        b += n
    return out


def subs_of(n):
    out, o = [], 0
    while o < n:
        m = min(512, n - o)
        out.append((o, m))
        o += m
    return out


class Builder:
    def __init__(self, nlayers=4, debug_x=False):
        self.nlayers = nlayers
        self.debug_x = debug_x
        nc = self.nc = bass.Bass("TRN2", target_bir_lowering=False)
        S = self.S = Sched(nc)
        di = lambda name, shape, dt=F32: nc.dram_tensor(name, shape, dt, kind="ExternalInput").ap()
        self.xT = di("xT", [KC, 128, NT])
        self.gains = di("gains", [128, 192])
        self.consts = di("consts", [128, 128 + 64 + 128])
        self.idb = di("idb", [128, 128], BF16)
        self.pool_w_in = di("pool_w_in", [2, D, D])
        self.pool_w_group = di("pool_w_group", [2, 4, 512, 512])
        self.pool_w_out = di("pool_w_out", [2, D, D])
        self.mlp_w_up = di("mlp_w_up", [4, D, 4 * D])
        self.mlp_w_down = di("mlp_w_down", [4, 4 * D, D])
        if nlayers > 1:
            self.sgu_w_in = di("sgu_w_in", [1, D, 2 * D])
            self.sgu_w_out = di("sgu_w_out", [1, D, D])
            self.wsT = di("wsT", [128, 8 * 128])
            self.bsb = di("bsb", [128, 8 * 128])
        if nlayers > 2:
            self.attn_w_qkv = di("attn_w_qkv", [1, D, 9216])
            self.attn_w_out = di("attn_w_out", [1, 1024, D])
            self.cmask = di("cmask", [128, NSC])
            self.valt = di("valt", [128, NB * NKB])
            self.QTD = nc.dram_tensor("QTD", [24, 128, NT], BF16, kind="Internal").ap()
            self.KTD = nc.dram_tensor("KTD", [24, 128, NT], BF16, kind="Internal").ap()
            self.VD = nc.dram_tensor("VD", [24, NB, 128, 128], BF16, kind="Internal").ap()
        self.XR = nc.dram_tensor("XR", [KC, 128, NT], F32, kind="Internal").ap()
        if debug_x:
            self.OUT = nc.dram_tensor("OUT", [KC, 128, NT], F32, kind="ExternalOutput").ap()
        else:
            self.OUT = nc.dram_tensor("OUT", [KC, 128, OWN], F32, kind="ExternalOutput").ap()

        AW = 53200
        ar = self.ar = S.sb("arena", [128, AW], F32)
        self.off = 0

        def carve(words):
            o = self.off
            self.off += words
            assert self.off <= AW, self.off
            return ar[:, o:o + words]
        self.ONES = carve(128)
        self.ICNT = carve(64).rearrange("p (g c) -> p g c", g=4)
        self.TRIL = carve(128)
        self.GAINS = carve(192)
        self.IDB = carve(64).bitcast(BF16)
        self.ONESB = carve(64).bitcast(BF16)
        self.SQ = [carve(512) for _ in range(2)]
        self.RS = carve(512)
        self.HH = [carve(528) for _ in range(2)]
        self.ST = [carve(528) for _ in range(2)]
        self.CARRY = carve(256).rearrange("p (k c) -> p k c", k=16)
        self.T16 = carve(16)
        self.TMP = [carve(512) for _ in range(2)]
        self.SMALL = carve(64)
        self.XA = carve(KC * TM).rearrange("p (k t) -> p k t", k=KC)
        self.GBw = carve(KC * TM // 2)
        self.GB = self.GBw.bitcast(BF16).rearrange("p (k t) -> p k t", k=KC)
        self.AO = self.GBw[:, 0:8 * TM // 2].bitcast(BF16).rearrange("p (k t) -> p k t", k=8)
        self.phase_base = self.off
        self.PS = S.ps("ps", [128, 4096], F32)
        self.bank_i = 0
        self.wr_i = 0
        self.sq_i = 0
        self.tmp_i = 0
        self.hh_i = 0
        self.carve = carve
        self.carve_linear()

    def carve_linear(self):
        self.off = self.phase_base
        c = self.carve
        self.HB = c(KC * TM // 2).bitcast(BF16).rearrange("p (k t) -> p k t", k=KC)
        self.WR = [c(4096).bitcast(BF16) for _ in range(3)]
        self.MB = c(4096).bitcast(BF16)
        self.WSB = c(512).bitcast(BF16)
        self.WSP = [c(512).bitcast(BF16) for _ in range(1)]
        self.BS = c(1024)
        self.SSQ = c(32)
        self.RV = c(8)

    def carve_attn(self):
        self.off = self.phase_base
        c = self.carve
        self.QH = [c(3 * TM // 2).bitcast(BF16).rearrange("p (g t) -> p g t", g=3) for _ in range(2)]
        klen = [TM + w for w in GW]
        self.KH = [[c(klen[g] // 2).bitcast(BF16) for g in range(3)] for _ in range(2)]
        self.VH = [[c((7 + GW[g] // 128) * 128 // 2).bitcast(BF16).rearrange("p (b d) -> p b d", d=128) for g in range(3)] for _ in range(2)]
        self.MASK = c(NSC)
        self.VALT = c(7 * NKB).rearrange("p (b j) -> p b j", j=NKB)
        self.SS = [c(NSC) for _ in range(1)]
        self.PP = [c(NSC // 2).bitcast(BF16) for _ in range(2)]
        self.PT = [c(NSC // 2).bitcast(BF16).rearrange("p (j q) -> p j q", q=128) for _ in range(2)]
        self.RD = c(128)

    def bank(self, n=512):
        b = self.bank_i
        self.bank_i = (b + 1) % 8
        return ("ps", b), self.PS[:, b * 512:b * 512 + n]

    def wslot(self):
        s = self.wr_i
        self.wr_i = (s + 1) % 3
        return ("wr", s), self.WR[s]

    def wload(self, src, a, b):
        key, slot = self.wslot()
        dst = slot[:, 0:a * b].rearrange("p (a b) -> p a b", a=a)
        self.S.add("pool", lambda e: e.dma_start(out=dst, in_=src), writes=[key], dma_key=key)
        return key, dst

    def gain(self, col):
        return self.GAINS[:, col:col + 1]

    def init(self):
        S = self.S
        S.add("sp", lambda e: e.dma_start(out=self.ONES, in_=self.consts[:, 0:128]), writes=["ones"], dma_key="c_ones")
        S.add("sp", lambda e: e.dma_start(out=self.ICNT, in_=self.consts[:, 128:192].rearrange("p (g c) -> p g c", g=4)), writes=["icnt"], dma_key="c_icnt")
        S.add("sp", lambda e: e.dma_start(out=self.TRIL, in_=self.consts[:, 192:320]), writes=["tril"], dma_key="c_tril")
        S.add("sp", lambda e: e.dma_start(out=self.GAINS, in_=self.gains), writes=["gains"], dma_key="c_gains")
        S.add("sp", lambda e: e.dma_start(out=self.IDB, in_=self.idb), writes=["idb"], dma_key="c_idb")
        S.add("dve", lambda e: e.tensor_copy(out=self.ONESB, in_=self.ONES), reads=["ones"], writes=["onesb"])
        S.add("dve", lambda e: e.memset(self.CARRY, 0.0), writes=[("carry", k) for k in range(KC)])

    def load_x(self, src, t0, n):
        S = self.S
        for si, (o, m) in enumerate(subs_of(n)):
            S.add("sp", lambda e, o=o, m=m: e.dma_start(out=self.XA[:, :, o:o + m], in_=src.rearrange("k p t -> p k t")[:, :, t0 + o:t0 + o + m]),
                  writes=[("xa", k, si) for k in range(KC)], dma_key=("xa", si))

    def store_x(self, dst, t0, n, c0=0):
        S = self.S
        for si, (o, m) in enumerate(subs_of(n)):
            S.add("sp", lambda e, o=o, m=m: e.dma_start(out=dst.rearrange("k p t -> p k t")[:, :, t0 + o:t0 + o + m], in_=self.XA[:, :, c0 + o:c0 + o + m]),
                  reads=[("xa", k, si) for k in range(KC)], dma_key=("xa", si))

    def norm(self, n, gcol, inplace=False):
        for si, (o, m) in enumerate(subs_of(n)):
            self._norm_sub(si, o, m, gcol, inplace)

    def _norm_sub(self, si, o, m, gcol, inplace):
        S = self.S
        bk, bank = self.bank(m)
        for kc in range(KC):
            sk = self.sq_i
            self.sq_i ^= 1
            sq = self.SQ[sk].bitcast(BF16)[:, 0:m]
            S.add("act", lambda e, kc=kc, sq=sq: e.activation(out=sq, in_=self.XA[:, kc, o:o + m], func=AF.Square),
                  reads=[("xa", kc, si)], writes=[("sq", sk)])
            S.add("pe", lambda e, kc=kc, sq=sq: e.matmul(bank, lhsT=self.ONESB, rhs=sq, start=(kc == 0), stop=(kc == KC - 1)),
                  reads=[("sq", sk), "onesb"], writes=[bk])
        rs = self.RS[:, 0:m]
        S.add("act", lambda e: e.activation(out=rs, in_=bank, func=AF.Sqrt, scale=1.0 / D, bias=EPS), reads=[bk], writes=["rs"])
        S.add("dve", lambda e: e.reciprocal(out=rs, in_=rs), reads=["rs"], writes=["rs"])
        for kc in range(KC):
            if inplace:
                dst, wk = self.XA[:, kc, o:o + m], ("xa", kc, si)
            else:
                dst, wk = self.HB[:, kc, o:o + m], ("hb", kc, si)
            S.add("dve", lambda e, kc=kc, dst=dst: e.scalar_tensor_tensor(out=dst, in0=self.XA[:, kc, o:o + m], scalar=self.gain(gcol + kc),
                                                                         in1=rs, op0=ALU.mult, op1=ALU.mult),
                  reads=[("xa", kc, si), "rs", "gains"], writes=[wk])

    def linear_a(self, inbuf, inkey, W, col0, ncols, n, evac):
        S = self.S
        Wv = W.rearrange("(k p) c -> p k c", p=128)
        for cb in range(ncols // 512):
            wk, w = self.wload(Wv[:, :, col0 + cb * 512: col0 + cb * 512 + 512], KC, 512)
            for si, (o, m) in enumerate(subs_of(n)):
                for oc in range(4):
                    bk, bank = self.bank(m)

                    def mm(e, oc=oc, o=o, m=m, bank=bank, w=w):
                        for kc in range(KC):
                            r = e.matmul(bank, lhsT=w[:, kc, oc * 128:(oc + 1) * 128], rhs=inbuf[:, kc, o:o + m], start=(kc == 0), stop=(kc == KC - 1))
                        return r
                    S.add("pe", mm, reads=[wk] + [(inkey, kc, si) for kc in range(KC)], writes=[bk])
                    evac(cb * 4 + oc, si, o, m, bk, bank)

    def evac_addx(self, OC, si, o, m, bk, bank):
        dst = self.XA[:, OC, o:o + m]
        self.S.add("dve", lambda e: e.tensor_tensor(out=dst, in0=dst, in1=bank, op=ALU.add), reads=[bk, ("xa", OC, si)], writes=[("xa", OC, si)])

    def mlp(self, layer, n):
        S = self.S
        self.norm(n, 64 + layer * 16)
        Wu = self.mlp_w_up[layer].rearrange("(k p) c -> p k c", p=128)
        Wd = self.mlp_w_down[layer].rearrange("(k p) c -> p k c", p=128)
        HID = [self.MB[:, s * 4 * TM:(s + 1) * 4 * TM].rearrange("p (k t) -> p k t", k=4) for s in range(2)]
        subs = subs_of(n)

        def up(j):
            wk, w = self.wload(Wu[:, :, j * 512:(j + 1) * 512], KC, 512)
            hs = j % 2
            for si, (o, m) in enumerate(subs):
                for oc in range(4):
                    bk, bank = self.bank(m)

                    def mm(e, oc=oc, o=o, m=m, bank=bank, w=w):
                        for kc in range(KC):
                            r = e.matmul(bank, lhsT=w[:, kc, oc * 128:(oc + 1) * 128], rhs=self.HB[:, kc, o:o + m], start=(kc == 0), stop=(kc == KC - 1))
                        return r
                    S.add("pe", mm, reads=[wk] + [("hb", kc, si) for kc in range(KC)], writes=[bk])
                    ti = self.tmp_i
                    self.tmp_i ^= 1
                    tmp = self.TMP[ti][:, 0:m]
                    S.add("act", lambda e, tmp=tmp, bank=bank: e.activation(out=tmp, in_=bank, func=AF.Relu), reads=[bk], writes=[("tmp", ti)])
                    dst = HID[hs][:, oc, o:o + m]
                    S.add("act", lambda e, tmp=tmp, dst=dst: e.activation(out=dst, in_=tmp, func=AF.Square), reads=[("tmp", ti)], writes=[("hid", hs, oc, si)])

        def down(j):
            wk, w = self.wload(Wd[:, j * 4:(j + 1) * 4, :], 4, D)
            hs = j % 2
            for si, (o, m) in enumerate(subs):
                for oc in range(KC):
                    bk, bank = self.bank(m)

                    def mm(e, oc=oc, o=o, m=m, bank=bank, w=w):
                        for kc in range(4):
                            r = e.matmul(bank, lhsT=w[:, kc, oc * 128:(oc + 1) * 128], rhs=HID[hs][:, kc, o:o + m], start=(kc == 0), stop=(kc == 3))
                        return r
                    S.add("pe", mm, reads=[wk] + [("hid", hs, kc, si) for kc in range(4)], writes=[bk])
                    self.evac_addx(oc, si, o, m, bk, bank)
        for j in range(16):
            up(j)
            if j >= 1:
                down(j - 1)
        down(15)

    def pool_mixer(self, j, layer, n, carry_only, fix_col):
        S = self.S
        self.norm(n, layer * 16)
        Wv = self.pool_w_in[j].rearrange("(k p) c -> p k c", p=128)
        subs = subs_of(n)
        for cb in range(4):
            wk, w = self.wload(Wv[:, :, cb * 512:(cb + 1) * 512], KC, 512)
            g = cb
            win = 2 << g
            for oc in range(4):
                OC = cb * 4 + oc
                for si, (o, m) in enumerate(subs):
                    bk, bank = self.bank(m)

                    def mm(e, oc=oc, o=o, m=m, bank=bank, w=w):
                        for kc in range(KC):
                            r = e.matmul(bank, lhsT=w[:, kc, oc * 128:(oc + 1) * 128], rhs=self.HB[:, kc, o:o + m], start=(kc == 0), stop=(kc == KC - 1))
                        return r
                    S.add("pe", mm, reads=[wk] + [("hb", kc, si) for kc in range(KC)], writes=[bk])
                    hi = self.hh_i
                    self.hh_i ^= 1
                    hh = self.HH[hi]
                    W_ = 16 + m
                    S.add("act", lambda e, hh=hh, bank=bank, m=m: e.activation(out=hh[:, 16:16 + m], in_=bank, func=AF.Copy), reads=[bk], writes=[("hh", hi)])
                    S.add("dve", lambda e, hh=hh, OC=OC: e.tensor_copy(out=hh[:, 0:16], in_=self.CARRY[:, OC, :]), reads=[("carry", OC)], writes=[("hh", hi)])
                    S.add("dve", lambda e, hh=hh, OC=OC, m=m: e.tensor_copy(out=self.CARRY[:, OC, :], in_=hh[:, m:m + 16]), reads=[("hh", hi)], writes=[("carry", OC)])
                    if carry_only:
                        continue
                    st0, st1 = self.ST
                    S.add("dve", lambda e, hh=hh, W_=W_: e.tensor_tensor(out=st0[:, 1:W_], in0=hh[:, 1:W_], in1=hh[:, 0:W_ - 1], op=ALU.add), reads=[("hh", hi)], writes=[("st", 0)])
                    cur, curk = st0, ("st", 0)
                    if g >= 1:
                        S.add("dve", lambda e, W_=W_: e.tensor_tensor(out=st1[:, 3:W_], in0=st0[:, 3:W_], in1=st0[:, 1:W_ - 2], op=ALU.add), reads=[("st", 0)], writes=[("st", 1)])
                        cur, curk = st1, ("st", 1)
                    if g >= 2:
                        S.add("dve", lambda e, W_=W_: e.tensor_tensor(out=st0[:, 7:W_], in0=st1[:, 7:W_], in1=st1[:, 3:W_ - 4], op=ALU.add), reads=[("st", 1)], writes=[("st", 0)])
                        cur, curk = st0, ("st", 0)
                    if g >= 3:
                        S.add("dve", lambda e, W_=W_: e.tensor_tensor(out=st1[:, 15:W_], in0=st0[:, 15:W_], in1=st0[:, 7:W_ - 8], op=ALU.add), reads=[("st", 0)], writes=[("st", 1)])
                        cur, curk = st1, ("st", 1)
                    dst = self.GB[:, OC, o:o + m]
                    S.add("dve", lambda e, cur=cur, hh=hh, dst=dst, m=m, win=win: e.scalar_tensor_tensor(out=dst, in0=cur[:, 16:16 + m], scalar=1.0 / win, in1=hh[:, 16:16 + m],
                                                                                                   op0=ALU.mult, op1=ALU.subtract),
                          reads=[curk, ("hh", hi)], writes=[("gb", OC, si)])
                    if fix_col is not None and o <= fix_col < o + m:
                        c0 = fix_col - o
                        S.add("dve", lambda e, cur=cur, c0=c0, g=g: e.tensor_tensor(out=self.T16, in0=cur[:, 16 + c0:32 + c0], in1=self.ICNT[:, g, :], op=ALU.mult),
                              reads=[curk, "icnt"], writes=["t16"])
                        S.add("dve", lambda e, hh=hh, c0=c0, OC=OC, o=o: e.tensor_tensor(out=self.GB[:, OC, o + c0:o + c0 + 16], in0=self.T16, in1=hh[:, 16 + c0:32 + c0], op=ALU.subtract),
                              reads=["t16", ("hh", hi)], writes=[("gb", OC, si)])
        if carry_only:
            return
        for g in range(4):
            wk, w = self.wload(self.pool_w_group[j, g].rearrange("(k p) c -> p k c", p=128), 4, 512)
            for oc in range(4):
                OC = g * 4 + oc
                for si, (o, m) in enumerate(subs):
                    bk, bank = self.bank(m)

                    def mm(e, oc=oc, o=o, m=m, bank=bank, w=w, g=g):
                        for kc in range(4):
                            r = e.matmul(bank, lhsT=w[:, kc, oc * 128:(oc + 1) * 128], rhs=self.GB[:, g * 4 + kc, o:o + m], start=(kc == 0), stop=(kc == 3))
                        return r
                    S.add("pe", mm, reads=[wk] + [("gb", g * 4 + kc, si) for kc in range(4)], writes=[bk])
                    dst = self.HB[:, OC, o:o + m]
                    S.add("act", lambda e, dst=dst, bank=bank, OC=OC: e.activation(out=dst, in_=bank, func=AF.Copy, scale=self.gain(144 + j * 16 + OC)),
                          reads=[bk, "gains"], writes=[("hb", OC, si)])
        self.linear_a(self.HB, "hb", self.pool_w_out[j], 0, D, n, self.evac_addx)

    def sgu_init(self):
        S = self.S
        for hf in range(2):
            S.add("sp", lambda e, hf=hf: e.dma_start(out=self.TMP[hf], in_=self.wsT[:, hf * 512:(hf + 1) * 512]), writes=[("tmp", hf)], dma_key=("tmpd", hf))
            for gg in range(4):
                g = hf * 4 + gg
                S.add("dve", lambda e, hf=hf, gg=gg, g=g: e.tensor_tensor(out=self.WSB[:, g * 128:(g + 1) * 128], in0=self.TMP[hf][:, gg * 128:(gg + 1) * 128],
                                                                          in1=self.TRIL, op=ALU.mult), reads=[("tmp", hf), "tril"], writes=["wsb"])
        S.add("sp", lambda e: e.dma_start(out=self.BS, in_=self.bsb), writes=["bs"], dma_key="c_bs")

    def sgu_mixer(self, layer, n):
        S = self.S
        self.norm(n, layer * 16)
        W = self.sgu_w_in[0]

        def evac_gelu(OC, si, o, m, bk, bank):
            dst = self.GB[:, OC, o:o + m]
            S.add("act", lambda e: e.activation(out=dst, in_=bank, func=AF.Gelu), reads=[bk], writes=[("gb", OC, si)])
        self.linear_a(self.HB, "hb", W, 0, D, n, evac_gelu)
        Wv = W.rearrange("(k p) c -> p k c", p=128)
        V = self.MB.rearrange("p (b c) -> p b c", b=4)
        nb = n // 128
        for g0 in range(0, nb, 4):
            tbs = list(range(g0, min(nb, g0 + 4)))
            S.add("dve", lambda e: e.memset(self.SSQ, 0.0), writes=["ssq"])
            for cb in range(4):
                wk, w = self.wload(Wv[:, :, D + cb * 512: D + (cb + 1) * 512], KC, 512)
                for ti, tb in enumerate(tbs):
                    self._sgu_v(wk, w, cb, ti, tb, V)
            for ti, tb in enumerate(tbs):
                self._sgu_sp(ti, tb, V)
        self.linear_a(self.GB, "gb", self.sgu_w_out[0], 0, D, n, self.evac_addx)

    def _sgu_v(self, wk, w, cb, ti, tb, V):
        S = self.S
        si = (tb * 128) // 512
        bk, bank = self.bank(512)

        def mm(e):
            for kc in range(KC):
                r = e.matmul(bank, lhsT=self.HB[:, kc, tb * 128:(tb + 1) * 128], rhs=w[:, kc, :], start=(kc == 0), stop=(kc == KC - 1))
            return r
        S.add("pe", mm, reads=[wk] + [("hb", kc, si) for kc in range(KC)], writes=[bk])
        t_i = self.tmp_i
        self.tmp_i ^= 1
        tmp = self.TMP[t_i]
        S.add("act", lambda e: e.activation(out=tmp, in_=bank, func=AF.Gelu), reads=[bk], writes=[("tmp", t_i)])
        sk = self.sq_i
        self.sq_i ^= 1
        col = ti * 4 + cb
        S.add("act", lambda e: e.activation(out=self.SQ[sk], in_=tmp, func=AF.Square, accum_out=self.SSQ[:, col:col + 1]),
              reads=[("tmp", t_i), "ssq"], writes=[("sq", sk), ("ssqc", col)])
        S.add("dve", lambda e: e.tensor_copy(out=V[:, ti, cb * 512:(cb + 1) * 512], in_=tmp), reads=[("tmp", t_i)], writes=[("v", ti, cb)])

    def _sgu_sp(self, ti, tb, V):
        S = self.S
        si = (tb * 128) // 512
        rv = self.RV[:, ti:ti + 1]
        S.add("dve", lambda e: e.reduce_sum(out=rv, in_=self.SSQ[:, ti * 4:(ti + 1) * 4], axis=AX.X),
              reads=[("ssqc", ti * 4 + c) for c in range(4)] + ["ssq"], writes=[("rv", ti)])
        S.add("act", lambda e: e.activation(out=rv, in_=rv, func=AF.Sqrt, scale=1.0 / D, bias=EPS), reads=[("rv", ti)], writes=[("rv", ti)])
        S.add("dve", lambda e: e.reciprocal(out=rv, in_=rv), reads=[("rv", ti)], writes=[("rv", ti)])
        wsp = self.WSP[0]
        S.add("dve", lambda e: e.tensor_scalar(out=wsp, in0=self.WSB, scalar1=rv, scalar2=None, op0=ALU.mult), reads=["wsb", ("rv", ti)], writes=["wsp"])
        for q4 in range(4):
            bk, bank = self.bank(512)

            def mm(e, q4=q4, bank=bank):
                for x in range(4):
                    cc = q4 * 4 + x
                    g = cc // 2
                    r = e.matmul(bank[:, x * 128:(x + 1) * 128], lhsT=V[:, ti, cc * 128:(cc + 1) * 128], rhs=wsp[:, g * 128:(g + 1) * 128], start=True, stop=True)
                return r
            S.add("pe", mm, reads=["wsp"] + [("v", ti, c) for c in range(4)], writes=[bk])
            for x in range(4):
                cc = q4 * 4 + x
                g = cc // 2
                sk = self.sq_i
                self.sq_i ^= 1
                t32 = self.SQ[sk][:, 0:128]
                S.add("dve", lambda e, x=x, cc=cc, g=g, t32=t32, bank=bank: e.scalar_tensor_tensor(out=t32, in0=bank[:, x * 128:(x + 1) * 128], scalar=self.gain(176 + cc),
                                                                                                   in1=self.BS[:, g * 128:(g + 1) * 128], op0=ALU.mult, op1=ALU.add),
                      reads=[bk, "bs", "gains"], writes=[("sq", sk)])
                dst = self.GB[:, cc, tb * 128:(tb + 1) * 128]
                S.add("dve", lambda e, t32=t32, dst=dst: e.tensor_tensor(out=dst, in0=t32, in1=dst, op=ALU.mult), reads=[("sq", sk), ("gb", cc, si)], writes=[("gb", cc, si)])

    def attn_mixer(self, layer, b0, nb, kv_only):
        S = self.S
        n, t0 = nb * 128, b0 * 128
        self.norm(n, layer * 16)
        W = self.attn_w_qkv[0]
        Wv = W.rearrange("(k p) c -> p k c", p=128)

        def proj_fm(col0, DST):
            def evac(OC, si, o, m, bk, bank):
                slot = OC % 16
                dst = self.GB[:, slot, o:o + m]
                if OC % 2 == 0:
                    S.add("act", lambda e: e.activation(out=dst, in_=bank, func=AF.Copy), reads=[bk], writes=[("gb", slot, si)])
                else:
                    S.add("dve", lambda e: e.tensor_copy(out=dst, in_=bank), reads=[bk], writes=[("gb", slot, si)])
                if OC % 4 == 3 and si == len(subs_of(n)) - 1:
                    c0 = OC - 3
                    s0 = c0 % 16
                    S.add("sp", lambda e: e.dma_start(out=DST[c0:c0 + 4].rearrange("c p t -> p c t")[:, :, t0:t0 + n], in_=self.GB[:, s0:s0 + 4, 0:n]),
                          reads=[("gb", s0 + i, s) for i in range(4) for s in range(2)], dma_key=("gbst", s0 // 4))
            self.linear_a(self.HB, "hb", W, col0, 3072, n, evac)
        proj_fm(3072, self.KTD)
        if not kv_only:
            proj_fm(0, self.QTD)
        VST = self.MB[:, 0:nb * 512].rearrange("p (b c) -> p b c", b=nb)
        for cb in range(6):
            wk, w = self.wload(Wv[:, :, 6144 + cb * 512: 6144 + (cb + 1) * 512], KC, 512)
            for ti in range(nb):
                self._attn_v(wk, w, ti, VST)
            for h4 in range(4):
                S.add("sp", lambda e, cb=cb, h4=h4: e.dma_start(out=self.VD[cb * 4 + h4, b0:b0 + nb].rearrange("b p d -> p b d"),
                                                                in_=VST[:, :, h4 * 128:(h4 + 1) * 128]),
                      reads=[("vst", ti) for ti in range(nb)], dma_key="vst")
        if kv_only:
            return
        S.barrier()
        self.carve_attn()
        S.add("sp", lambda e: e.dma_start(out=self.MASK, in_=self.cmask), writes=["mask"], dma_key="mask")
        S.add("sp", lambda e: e.dma_start(out=self.VALT[:, 0:nb, :], in_=self.valt.rearrange("p (b j) -> p b j", j=NKB)[:, b0:b0 + nb, :]), writes=["valt"], dma_key="valt")
        it = 0
        prev = None
        for hd in range(8):
            sl = hd % 2
            QTv = self.QTD.rearrange("(g h) p t -> h p g t", h=8)
            S.add("sp", lambda e, hd=hd, sl=sl: e.dma_start(out=self.QH[sl][:, :, 0:n], in_=QTv[hd][:, :, t0:t0 + n]), writes=[("qh", sl)], dma_key=("qh", sl))
            for g in range(3):
                wb = GW[g] // 128
                S.add("sp", lambda e, hd=hd, sl=sl, g=g: e.dma_start(out=self.KH[sl][g][:, 0:n + GW[g]], in_=self.KTD[g * 8 + hd][:, t0 - GW[g]:t0 + n]),
                      writes=[("kh", sl, g)], dma_key=("kh", sl, g))
                S.add("sp", lambda e, hd=hd, sl=sl, g=g, wb=wb: e.dma_start(out=self.VH[sl][g][:, 0:nb + wb, :], in_=self.VD[g * 8 + hd, b0 - wb:b0 + nb].rearrange("b p d -> p b d")),
                      writes=[("vh", sl, g)], dma_key=("vh", sl, g))
            for qb in range(nb):
                self._attn_core(hd, sl, qb, it % 2)
                if prev is not None:
                    self._attn_b(*prev)
                prev = (hd, sl, qb, it % 2)
                it += 1
        self._attn_b(*prev)
        S.barrier()
        self.carve_linear()
        Wo = self.attn_w_out[0].rearrange("(k p) c -> p k c", p=128)
        wk0, w0 = self.wload(Wo[:, 0:4, :], 4, D)
        wk1, w1 = self.wload(Wo[:, 4:8, :], 4, D)
        for si, (o, m) in enumerate(subs_of(n)):
            for oc in range(KC):
                bk, bank = self.bank(m)

                def mm(e, oc=oc, o=o, m=m, bank=bank):
                    for kc in range(8):
                        w = w0 if kc < 4 else w1
                        r = e.matmul(bank, lhsT=w[:, kc % 4, oc * 128:(oc + 1) * 128], rhs=self.GB[:, kc, o:o + m], start=(kc == 0), stop=(kc == 7))
                    return r
                S.add("pe", mm, reads=[wk0, wk1] + [("gb", kc, si) for kc in range(8)], writes=[bk])
                self.evac_addx(oc, si, o, m, bk, bank)

    def _attn_v(self, wk, w, ti, VST):
        S = self.S
        si = (ti * 128) // 512
        bk, bank = self.bank(512)

        def mm(e):
            for kc in range(KC):
                r = e.matmul(bank, lhsT=self.HB[:, kc, ti * 128:(ti + 1) * 128], rhs=w[:, kc, :], start=(kc == 0), stop=(kc == KC - 1))
            return r
        S.add("pe", mm, reads=[wk] + [("hb", kc, si) for kc in range(KC)], writes=[bk])
        if ti % 2 == 0:
            S.add("act", lambda e: e.activation(out=VST[:, ti, :], in_=bank, func=AF.Copy), reads=[bk], writes=[("vst", ti)])
        else:
            S.add("dve", lambda e: e.tensor_copy(out=VST[:, ti, :], in_=bank), reads=[bk], writes=[("vst", ti)])

    def _attn_core(self, hd, sl, qb, ss):
        S = self.S
        PS = self.PS
        QH, KH, VH = self.QH[sl], self.KH[sl], self.VH[sl]
        q0 = qb * 128

        def scores(e):
            for g, kc0, ncol, pc0 in [(2, 0, 512, 0), (2, 512, 512, 512), (2, 1024, 512, 1024), (2, 1536, 512, 1536), (2, 2048, 128, 2048),
                                      (0, 0, 256, 2176), (1, 512, 128, 2432), (1, 0, 512, 2560)]:
                r = e.matmul(PS[:, pc0:pc0 + ncol], lhsT=QH[:, g, q0:q0 + 128], rhs=KH[g][:, q0 + kc0:q0 + kc0 + ncol], start=True, stop=True)
            return r
        S.add("pe", scores, reads=[("qh", sl)] + [("kh", sl, g) for g in range(3)], writes=[("ps", b) for b in range(6)])
        SSb, PP, PT = self.SS[0], self.PP[ss], self.PT[ss]
        S.add("dve", lambda e: e.tensor_tensor(out=SSb, in0=PS[:, 0:NSC], in1=self.MASK, op=ALU.add), reads=[("ps", b) for b in range(6)] + ["mask"], writes=[("ss", 0)])
        c = ss * 2
        mx = self.SMALL[:, c:c + 1]
        nbv = self.SMALL[:, c + 1:c + 2]
        S.add("dve", lambda e: e.reduce_max(out=mx, in_=SSb, axis=AX.X), reads=[("ss", 0)], writes=[("mx", ss)])
        S.add("dve", lambda e: e.tensor_scalar(out=nbv, in0=mx, scalar1=-SCALE, scalar2=None, op0=ALU.mult), reads=[("mx", ss)], writes=[("nb", ss)])
        S.add("act", lambda e: e.activation(out=PP, in_=SSb, func=AF.Exp, scale=SCALE, bias=nbv), reads=[("ss", 0), ("nb", ss)], writes=[("pp", ss)])

    def _attn_b(self, hd, sl, qb, ss):
        S = self.S
        PS = self.PS
        VH = self.VH[sl]
        q0 = qb * 128
        PP, PT = self.PP[ss], self.PT[ss]
        b6 = PS[:, 6 * 512:7 * 512].bitcast(BF16)
        for r_ in range(3):
            def tr(e, r_=r_):
                for x in range(8):
                    jx = r_ * 8 + x
                    r = e.transpose(b6[:, x * 128:(x + 1) * 128], PP[:, jx * 128:(jx + 1) * 128], self.IDB)
                return r
            S.add("pe", tr, reads=[("pp", ss), "idb"], writes=[("ps", 6)])
            vb = self.VALT[:, qb, r_ * 8:(r_ + 1) * 8].unsqueeze(2).to_broadcast([128, 8, 128])
            S.add("dve", lambda e, r_=r_, vb=vb: e.tensor_tensor(out=PT[:, r_ * 8:(r_ + 1) * 8, :], in0=b6.rearrange("p (j q) -> p j q", q=128), in1=vb, op=ALU.mult),
                  reads=[("ps", 6), "valt"], writes=[("pt", ss, r_)])
        O = PS[:, 7 * 512:7 * 512 + 128]
        DN = PS[:, 7 * 512 + 256:7 * 512 + 384]

        def pv(e):
            for jx, (g, kb) in enumerate(SC_ORDER):
                e.matmul(O, lhsT=VH[g][:, qb + kb, :], rhs=PT[:, jx, :], start=(jx == 0), stop=(jx == NKB - 1))
            for jx in range(NKB):
                r = e.matmul(DN, lhsT=self.ONESB, rhs=PT[:, jx, :], start=(jx == 0), stop=(jx == NKB - 1))
            return r
        S.add("pe", pv, reads=[("pt", ss, r_) for r_ in range(3)] + [("vh", sl, g) for g in range(3)] + ["onesb"], writes=[("ps", 7)])
        S.add("dve", lambda e: e.tensor_scalar(out=self.RD, in0=DN, scalar1=1e-30, scalar2=None, op0=ALU.add), reads=[("ps", 7)], writes=["rd"])
        S.add("dve", lambda e: e.reciprocal(out=self.RD, in_=self.RD), reads=["rd"], writes=["rd"])
        dst = self.GB[:, hd, q0:q0 + 128]
        si = q0 // 512
        S.add("dve", lambda e: e.tensor_tensor(out=dst, in0=O, in1=self.RD, op=ALU.mult), reads=[("ps", 7), "rd"], writes=[("gb", hd, si)])

    def build(self):
        S = self.S
        self.init()
        if self.nlayers > 1:
            self.sgu_init()
        L = self.nlayers
        last = L - 1
        for layer in range(L):
            kind = layer % 3
            j = layer // 3
            src = self.xT if layer == 0 else self.XR
            if layer == 0:
                tl = [(0, 1, True)] + [(b, n, False) for b, n in tiles_of(1, NB, 5)]
            elif layer == 1:
                tl = [(b, n, False) for b, n in tiles_of(1, NB, 5)]
            elif layer == 2:
                tl = [(b, n, True) for b, n in tiles_of(1, 17, 2)] + [(b, n, False) for b, n in tiles_of(17, NB, 3)]
            else:
                tl = [(17, 1, True)] + [(b, n, False) for b, n in tiles_of(18, NB, 2)]
            for (b0, nb, partial) in tl:
                t0, n = b0 * 128, nb * 128
                self.load_x(src, t0, n)
                if kind == 0:
                    fix = None
                    if b0 <= 18 < b0 + nb:
                        fix = (18 - b0) * 128
                    self.pool_mixer(j, layer, n, partial, fix)
                elif kind == 1:
                    self.sgu_mixer(layer, n)
                else:
                    self.attn_mixer(layer, b0, nb, partial)
                if partial:
                    continue
                self.mlp(layer, n)
                if layer == last and not self.debug_x:
                    self.norm(n, 128, inplace=True)
                    self.store_x(self.OUT, t0 - HALO, n)
                elif layer == last:
                    self.store_x(self.OUT, t0, n)
                else:
                    self.store_x(self.XR, t0, n)
        S.emit()
        return self.nc


def _feat(v):
    v = np.asarray(v, np.float32).reshape(-1, KC, 128)
    return np.ascontiguousarray(v.transpose(2, 0, 1)).reshape(128, -1)


def prep_shared(inp, nlayers=4):
    sh = {}
    g = np.concatenate([_feat(inp["norm_mix"]), _feat(inp["norm_mlp"]), _feat(inp["norm_final"]),
                        _feat(inp["pool_scale"]), _feat(inp["sgu_v_norm"])], axis=1)
    sh["gains"] = np.ascontiguousarray(g, np.float32)
    sh["idb"] = np.eye(128).astype(ml_dtypes.bfloat16)
    for k in ["pool_w_in", "pool_w_group", "pool_w_out", "mlp_w_up", "mlp_w_down"]:
        sh[k] = np.ascontiguousarray(inp[k], np.float32)
    if nlayers > 1:
        sh["sgu_w_in"] = np.ascontiguousarray(inp["sgu_w_in"], np.float32)
        sh["sgu_w_out"] = np.ascontiguousarray(inp["sgu_w_out"], np.float32)
        ws = np.asarray(inp["sgu_w_s"], np.float32)[0]
        sh["wsT"] = np.ascontiguousarray(ws.transpose(2, 0, 1)).reshape(128, 1024)
        bs = np.asarray(inp["sgu_b_s"], np.float32)[0]
        sh["bsb"] = np.ascontiguousarray(np.broadcast_to(bs.reshape(1, 1024), (128, 1024)))
    if nlayers > 2:
        sh["attn_w_qkv"] = np.ascontiguousarray(inp["attn_w_qkv"], np.float32)
        sh["attn_w_out"] = np.ascontiguousarray(inp["attn_w_out"], np.float32)
        cm = np.full((128, NSC), -1e30, np.float32)
        q = np.arange(128)[:, None]
        kk = np.arange(128)[None, :]
        for jx, (g_, kb) in enumerate(SC_ORDER):
            delta = q + GW[g_] - (kb * 128 + kk)
            ok = (delta >= 0) & (delta <= GW[g_]) & (delta % GD[g_] == 0)
            cm[:, jx * 128:(jx + 1) * 128] = np.where(ok, 0.0, -1e30)
        sh["cmask"] = cm
    return sh


def prep_core(inp, c, nlayers=4):
    pc = {}
    x = np.asarray(inp["x"], np.float32)[0]
    S0 = c * OWN
    xp = np.zeros((NT, D), np.float32)
    lo = S0 - HALO
    a = max(lo, 0)
    xp[a - lo:] = x[a:S0 + OWN]
    pc["xT"] = np.ascontiguousarray(xp.T).reshape(KC, 128, NT)
    icnt = np.zeros((4, 16), np.float32)
    for g_ in range(4):
        w = 2 << g_
        for t in range(16):
            icnt[g_, t] = 1.0 / min(t + 1, w) if c == 0 else 1.0 / w
    tril = (np.arange(128)[:, None] <= np.arange(128)[None, :]).astype(np.float32)
    pc["consts"] = np.ascontiguousarray(np.concatenate(
        [np.ones((128, 128), np.float32), np.broadcast_to(icnt.reshape(1, 64), (128, 64)), tril], axis=1))
    if nlayers > 2:
        valid_blk = np.array([1.0 if (lo + b * 128) >= 0 else 0.0 for b in range(NB)], np.float32)
        vt = np.zeros((NB, NKB), np.float32)
        for b in range(NB):
            for jx, (g_, kb) in enumerate(SC_ORDER):
                kblk = b - GW[g_] // 128 + kb
                vt[b, jx] = valid_blk[kblk] if kblk >= 0 else 0.0
        pc["valt"] = np.ascontiguousarray(np.broadcast_to(vt.reshape(1, -1), (128, NB * NKB)))
    return pc


def kernel(**inputs):
    b = Builder(4, False)
    nc = b.build()
    sh = prep_shared(inputs, 4)
    in_maps = []
    for c in range(NCORES):
        m = dict(sh)
        m.update(prep_core(inputs, c, 4))
        in_maps.append(m)
    res = run_bass_kernel_spmd(nc, in_maps, core_ids=list(range(NCORES)))
    outs = []
    for c in range(NCORES):
        o = np.asarray(res.results[c]["OUT"], np.float32).reshape(D, OWN)
        outs.append(o.T)
    return np.ascontiguousarray(np.concatenate(outs, axis=0)).reshape(1, NCORES * OWN, D).astype(np.float32)
```

```python
import contextlib
import numpy as np
import ml_dtypes
import concourse.bass as bass
import concourse.mybir as mybir
from concourse.bass_utils import run_bass_kernel_spmd

F32 = mybir.dt.float32
BF16 = mybir.dt.bfloat16
AF = mybir.ActivationFunctionType
ALU = mybir.AluOpType
AX = mybir.AxisListType

ENGS = ["pe", "act", "dve", "pool", "sp"]


class Sched:
    def __init__(self, nc):
        self.nc = nc
        self.ops = []
        self.eng_ops = {e: [] for e in ENGS}
        self.lastw = {}
        self.readers = {}
        self.dma_count = {}
        self.last_dma = {}
        self.stack = contextlib.ExitStack()

    def sb(self, name, shape, dt):
        return self.stack.enter_context(self.nc.sbuf_tensor(name, shape, dt))

    def ps(self, name, shape, dt):
        return self.stack.enter_context(self.nc.psum_tensor(name, shape, dt))

    def add(self, eng, fn, reads=(), writes=(), dma_key=None):
        oid = len(self.ops)
        deps = set()
        rk = list(reads)
        wk = list(writes)
        if dma_key is not None:
            wk.append(("dmaq", dma_key))
        for k in rk:
            if k in self.lastw:
                deps.add(self.lastw[k])
        for k in wk:
            if k in self.lastw:
                deps.add(self.lastw[k])
            for r in self.readers.get(k, ()):
                deps.add(r)
        for k in rk:
            self.readers.setdefault(k, []).append(oid)
        for k in wk:
            self.lastw[k] = oid
            self.readers[k] = []
        op = dict(id=oid, eng=eng, fn=fn, deps=deps, dma_key=dma_key, ms=None, need=False)
        if dma_key is not None:
            c = self.dma_count.get(dma_key, 0) + 1
            self.dma_count[dma_key] = c
            op["dma_cnt"] = c
            self.last_dma[dma_key] = oid
        self.ops.append(op)
        self.eng_ops[eng].append(op)
        return oid

    def barrier(self):
        last = set()
        for e in ENGS:
            for op in reversed(self.eng_ops[e]):
                if op["fn"] is not None and op["dma_key"] is None:
                    last.add(op["id"])
                    break
        for k, oid in self.last_dma.items():
            last.add(oid)
        for e in ENGS:
            oid = len(self.ops)
            op = dict(id=oid, eng=e, fn=None, deps=set(last), dma_key=None, ms=None, need=False)
            self.ops.append(op)
            self.eng_ops[e].append(op)
        self.lastw = {}
        self.readers = {}

    def emit(self, final_wait=True):
        nc = self.nc
        ops = self.ops
        if final_wait:
            self.barrier()
        for op in ops:
            for d in op["deps"]:
                dop = ops[d]
                if dop["dma_key"] is None:
                    dop["need"] = True
        cnt = {e: 0 for e in ENGS}
        for e in ENGS:
            for op in self.eng_ops[e]:
                if op["need"]:
                    cnt[e] += 1
                    op["ms"] = cnt[e]
        st = self.stack
        sem_e = {e: st.enter_context(nc.semaphore("se_" + e)) for e in ENGS}
        sem_d = {}
        for i, k in enumerate(self.dma_count):
            sem_d[k] = st.enter_context(nc.semaphore("sd_%d" % i))
        self.n_sems = len(sem_e) + len(sem_d)
        self.n_waits = 0
        block = st.enter_context(nc.Block())

        def body(ename):
            def f(eng):
                waited = {}
                for op in self.eng_ops[ename]:
                    for d in sorted(op["deps"]):
                        dop = ops[d]
                        if dop["dma_key"] is not None:
                            sem, val = sem_d[dop["dma_key"]], 16 * dop["dma_cnt"]
                        else:
                            if dop["eng"] == ename and ename == "pe":
                                continue
                            sem, val = sem_e[dop["eng"]], dop["ms"]
                        key = id(sem)
                        if waited.get(key, 0) >= val:
                            continue
                        waited[key] = val
                        eng.wait_ge(sem, val)
                        self.n_waits += 1
                    if op["fn"] is None:
                        continue
                    ins = op["fn"](eng)
                    if op["dma_key"] is not None:
                        ins.then_inc(sem_d[op["dma_key"]], 16)
                    elif op["ms"] is not None:
                        ins.then_inc(sem_e[ename], 1)
            return f

        block.tensor(body("pe"))
        block.scalar(body("act"))
        block.vector(body("dve"))
        block.gpsimd(body("pool"))
        block.sync(body("sp"))
        st.close()


NCORES = 8
D = 2048
KC = 16
OWN = 2048
NB = 34
NT = NB * 128
HALO = NT - OWN
TM = 896
EPS = 1e-6
SCALE = 128.0 ** -0.5
GW = (128, 512, 2048)
GD = (1, 4, 16)
SC_ORDER = [(2, k) for k in range(17)] + [(0, 0), (0, 1), (1, 4)] + [(1, k) for k in range(4)]
NKB = len(SC_ORDER)
NSC = NKB * 128


def tiles_of(b0, b1, first):
    out = [(b0, first)] if first else []
    b = b0 + first
    while b < b1:
        n = min(7, b1 - b)
        out.append((b, n))
        b += n
    return out


def subs_of(n):
    out, o = [], 0
    while o < n:
        m = min(512, n - o)
        out.append((o, m))
        o += m
    return out


class Builder:
    def __init__(self, nlayers=4, debug_x=False):
        self.nlayers = nlayers
        self.debug_x = debug_x
        nc = self.nc = bass.Bass("TRN2", target_bir_lowering=False)
        S = self.S = Sched(nc)
        di = lambda name, shape, dt=F32: nc.dram_tensor(name, shape, dt, kind="ExternalInput").ap()
        self.xT = di("xT", [KC, 128, NT])
        self.gains = di("gains", [128, 192])
        self.consts = di("consts", [128, 128 + 64 + 128])
        self.idb = di("idb", [128, 128], BF16)
        self.pool_w_in = di("pool_w_in", [2, D, D])
        self.pool_w_group = di("pool_w_group", [2, 4, 512, 512])
        self.pool_w_out = di("pool_w_out", [2, D, D])
        self.mlp_w_up = di("mlp_w_up", [4, D, 4 * D])
        self.mlp_w_down = di("mlp_w_down", [4, 4 * D, D])
        if nlayers > 1:
            self.sgu_w_in = di("sgu_w_in", [1, D, 2 * D])
            self.sgu_w_out = di("sgu_w_out", [1, D, D])
            self.wsT = di("wsT", [128, 8 * 128])
            self.bsb = di("bsb", [128, 8 * 128])
        if nlayers > 2:
            self.attn_w_qkv = di("attn_w_qkv", [1, D, 9216])
            self.attn_w_out = di("attn_w_out", [1, 1024, D])
            self.cmask = di("cmask", [128, NSC])
            self.valt = di("valt", [128, NB * NKB])
            self.QTD = nc.dram_tensor("QTD", [24, 128, NT], BF16, kind="Internal").ap()
            self.KTD = nc.dram_tensor("KTD", [24, 128, NT], BF16, kind="Internal").ap()
            self.VD = nc.dram_tensor("VD", [24, NB, 128, 128], BF16, kind="Internal").ap()
        self.XR = nc.dram_tensor("XR", [KC, 128, NT], F32, kind="Internal").ap()
        if debug_x:
            self.OUT = nc.dram_tensor("OUT", [KC, 128, NT], F32, kind="ExternalOutput").ap()
        else:
            self.OUT = nc.dram_tensor("OUT", [KC, 128, OWN], F32, kind="ExternalOutput").ap()

        AW = 53200
        ar = self.ar = S.sb("arena", [128, AW], F32)
        self.off = 0

        def carve(words):
            o = self.off
            self.off += words
            assert self.off <= AW, self.off
            return ar[:, o:o + words]
        self.ONES = carve(128)
        self.ICNT = carve(64).rearrange("p (g c) -> p g c", g=4)
        self.TRIL = carve(128)
        self.GAINS = carve(192)
        self.IDB = carve(64).bitcast(BF16)
        self.ONESB = carve(64).bitcast(BF16)
        self.SQ = [carve(512) for _ in range(2)]
        self.RS = carve(512)
        self.HH = [carve(528) for _ in range(2)]
        self.ST = [carve(528) for _ in range(2)]
        self.CARRY = carve(256).rearrange("p (k c) -> p k c", k=16)
        self.T16 = carve(16)
        self.TMP = [carve(512) for _ in range(2)]
        self.SMALL = carve(64)
        self.XA = carve(KC * TM).rearrange("p (k t) -> p k t", k=KC)
        self.GBw = carve(KC * TM // 2)
        self.GB = self.GBw.bitcast(BF16).rearrange("p (k t) -> p k t", k=KC)
        self.AO = self.GBw[:, 0:8 * TM // 2].bitcast(BF16).rearrange("p (k t) -> p k t", k=8)
        self.phase_base = self.off
        self.PS = S.ps("ps", [128, 4096], F32)
        self.bank_i = 0
        self.wr_i = 0
        self.sq_i = 0
        self.tmp_i = 0
        self.hh_i = 0
        self.carve = carve
        self.carve_linear()

    def carve_linear(self):
        self.off = self.phase_base
        c = self.carve
        self.HB = c(KC * TM // 2).bitcast(BF16).rearrange("p (k t) -> p k t", k=KC)
        self.WR = [c(4096).bitcast(BF16) for _ in range(3)]
        self.MB = c(4096).bitcast(BF16)
        self.WSB = c(512).bitcast(BF16)
        self.WSP = [c(512).bitcast(BF16) for _ in range(1)]
        self.BS = c(1024)
        self.SSQ = c(32)
        self.RV = c(8)

    def carve_attn(self):
        self.off = self.phase_base
        c = self.carve
        self.QH = [c(3 * TM // 2).bitcast(BF16).rearrange("p (g t) -> p g t", g=3) for _ in range(2)]
        klen = [TM + w for w in GW]
        self.KH = [[c(klen[g] // 2).bitcast(BF16) for g in range(3)] for _ in range(2)]
        self.VH = [[c((7 + GW[g] // 128) * 128 // 2).bitcast(BF16).rearrange("p (b d) -> p b d", d=128) for g in range(3)] for _ in range(2)]
        self.MASK = c(NSC)
        self.VALT = c(7 * NKB).rearrange("p (b j) -> p b j", j=NKB)
        self.SS = [c(NSC) for _ in range(1)]
        self.PP = [c(NSC // 2).bitcast(BF16) for _ in range(2)]
        self.PT = [c(NSC // 2).bitcast(BF16).rearrange("p (j q) -> p j q", q=128) for _ in range(2)]
        self.RD = c(128)

    def bank(self, n=512):
        b = self.bank_i
        self.bank_i = (b + 1) % 8
        return ("ps", b), self.PS[:, b * 512:b * 512 + n]

    def wslot(self):
        s = self.wr_i
        self.wr_i = (s + 1) % 3
        return ("wr", s), self.WR[s]

    def wload(self, src, a, b):
        key, slot = self.wslot()
        dst = slot[:, 0:a * b].rearrange("p (a b) -> p a b", a=a)
        self.S.add("pool", lambda e: e.dma_start(out=dst, in_=src), writes=[key], dma_key=key)
        return key, dst

    def gain(self, col):
        return self.GAINS[:, col:col + 1]

    def init(self):
        S = self.S
        S.add("sp", lambda e: e.dma_start(out=self.ONES, in_=self.consts[:, 0:128]), writes=["ones"], dma_key="c_ones")
        S.add("sp", lambda e: e.dma_start(out=self.ICNT, in_=self.consts[:, 128:192].rearrange("p (g c) -> p g c", g=4)), writes=["icnt"], dma_key="c_icnt")
        S.add("sp", lambda e: e.dma_start(out=self.TRIL, in_=self.consts[:, 192:320]), writes=["tril"], dma_key="c_tril")
        S.add("sp", lambda e: e.dma_start(out=self.GAINS, in_=self.gains), writes=["gains"], dma_key="c_gains")
        S.add("sp", lambda e: e.dma_start(out=self.IDB, in_=self.idb), writes=["idb"], dma_key="c_idb")
        S.add("dve", lambda e: e.tensor_copy(out=self.ONESB, in_=self.ONES), reads=["ones"], writes=["onesb"])
        S.add("dve", lambda e: e.memset(self.CARRY, 0.0), writes=[("carry", k) for k in range(KC)])

    def load_x(self, src, t0, n):
        S = self.S
        for si, (o, m) in enumerate(subs_of(n)):
            S.add("sp", lambda e, o=o, m=m: e.dma_start(out=self.XA[:, :, o:o + m], in_=src.rearrange("k p t -> p k t")[:, :, t0 + o:t0 + o + m]),
                  writes=[("xa", k, si) for k in range(KC)], dma_key=("xa", si))

    def store_x(self, dst, t0, n, c0=0):
        S = self.S
        for si, (o, m) in enumerate(subs_of(n)):
            S.add("sp", lambda e, o=o, m=m: e.dma_start(out=dst.rearrange("k p t -> p k t")[:, :, t0 + o:t0 + o + m], in_=self.XA[:, :, c0 + o:c0 + o + m]),
                  reads=[("xa", k, si) for k in range(KC)], dma_key=("xa", si))

    def norm(self, n, gcol, inplace=False):
        for si, (o, m) in enumerate(subs_of(n)):
            self._norm_sub(si, o, m, gcol, inplace)

    def _norm_sub(self, si, o, m, gcol, inplace):
        S = self.S
        bk, bank = self.bank(m)
        for kc in range(KC):
            sk = self.sq_i
            self.sq_i ^= 1
            sq = self.SQ[sk].bitcast(BF16)[:, 0:m]
            S.add("act", lambda e, kc=kc, sq=sq: e.activation(out=sq, in_=self.XA[:, kc, o:o + m], func=AF.Square),
                  reads=[("xa", kc, si)], writes=[("sq", sk)])
            S.add("pe", lambda e, kc=kc, sq=sq: e.matmul(bank, lhsT=self.ONESB, rhs=sq, start=(kc == 0), stop=(kc == KC - 1)),
                  reads=[("sq", sk), "onesb"], writes=[bk])
        rs = self.RS[:, 0:m]
        S.add("act", lambda e: e.activation(out=rs, in_=bank, func=AF.Sqrt, scale=1.0 / D, bias=EPS), reads=[bk], writes=["rs"])
        S.add("dve", lambda e: e.reciprocal(out=rs, in_=rs), reads=["rs"], writes=["rs"])
        for kc in range(KC):
            if inplace:
                dst, wk = self.XA[:, kc, o:o + m], ("xa", kc, si)
            else:
                dst, wk = self.HB[:, kc, o:o + m], ("hb", kc, si)
            S.add("dve", lambda e, kc=kc, dst=dst: e.scalar_tensor_tensor(out=dst, in0=self.XA[:, kc, o:o + m], scalar=self.gain(gcol + kc),
                                                                         in1=rs, op0=ALU.mult, op1=ALU.mult),
                  reads=[("xa", kc, si), "rs", "gains"], writes=[wk])

    def linear_a(self, inbuf, inkey, W, col0, ncols, n, evac):
        S = self.S
        Wv = W.rearrange("(k p) c -> p k c", p=128)
        for cb in range(ncols // 512):
            wk, w = self.wload(Wv[:, :, col0 + cb * 512: col0 + cb * 512 + 512], KC, 512)
            for si, (o, m) in enumerate(subs_of(n)):
                for oc in range(4):
                    bk, bank = self.bank(m)

                    def mm(e, oc=oc, o=o, m=m, bank=bank, w=w):
                        for kc in range(KC):
                            r = e.matmul(bank, lhsT=w[:, kc, oc * 128:(oc + 1) * 128], rhs=inbuf[:, kc, o:o + m], start=(kc == 0), stop=(kc == KC - 1))
                        return r
                    S.add("pe", mm, reads=[wk] + [(inkey, kc, si) for kc in range(KC)], writes=[bk])
                    evac(cb * 4 + oc, si, o, m, bk, bank)

    def evac_addx(self, OC, si, o, m, bk, bank):
        dst = self.XA[:, OC, o:o + m]
        self.S.add("dve", lambda e: e.tensor_tensor(out=dst, in0=dst, in1=bank, op=ALU.add), reads=[bk, ("xa", OC, si)], writes=[("xa", OC, si)])

    def mlp(self, layer, n):
        S = self.S
        self.norm(n, 64 + layer * 16)
        Wu = self.mlp_w_up[layer].rearrange("(k p) c -> p k c", p=128)
        Wd = self.mlp_w_down[layer].rearrange("(k p) c -> p k c", p=128)
        HID = [self.MB[:, s * 4 * TM:(s + 1) * 4 * TM].rearrange("p (k t) -> p k t", k=4) for s in range(2)]
        subs = subs_of(n)

        def up(j):
            wk, w = self.wload(Wu[:, :, j * 512:(j + 1) * 512], KC, 512)
            hs = j % 2
            for si, (o, m) in enumerate(subs):
                for oc in range(4):
                    bk, bank = self.bank(m)

                    def mm(e, oc=oc, o=o, m=m, bank=bank, w=w):
                        for kc in range(KC):
                            r = e.matmul(bank, lhsT=w[:, kc, oc * 128:(oc + 1) * 128], rhs=self.HB[:, kc, o:o + m], start=(kc == 0), stop=(kc == KC - 1))
                        return r
                    S.add("pe", mm, reads=[wk] + [("hb", kc, si) for kc in range(KC)], writes=[bk])
                    ti = self.tmp_i
                    self.tmp_i ^= 1
                    tmp = self.TMP[ti][:, 0:m]
                    S.add("act", lambda e, tmp=tmp, bank=bank: e.activation(out=tmp, in_=bank, func=AF.Relu), reads=[bk], writes=[("tmp", ti)])
                    dst = HID[hs][:, oc, o:o + m]
                    S.add("act", lambda e, tmp=tmp, dst=dst: e.activation(out=dst, in_=tmp, func=AF.Square), reads=[("tmp", ti)], writes=[("hid", hs, oc, si)])

        def down(j):
            wk, w = self.wload(Wd[:, j * 4:(j + 1) * 4, :], 4, D)
            hs = j % 2
            for si, (o, m) in enumerate(subs):
                for oc in range(KC):
                    bk, bank = self.bank(m)

                    def mm(e, oc=oc, o=o, m=m, bank=bank, w=w):
                        for kc in range(4):
                            r = e.matmul(bank, lhsT=w[:, kc, oc * 128:(oc + 1) * 128], rhs=HID[hs][:, kc, o:o + m], start=(kc == 0), stop=(kc == 3))
                        return r
                    S.add("pe", mm, reads=[wk] + [("hid", hs, kc, si) for kc in range(4)], writes=[bk])
                    self.evac_addx(oc, si, o, m, bk, bank)
        for j in range(16):
            up(j)
            if j >= 1:
                down(j - 1)
        down(15)

    def pool_mixer(self, j, layer, n, carry_only, fix_col):
        S = self.S
        self.norm(n, layer * 16)
        Wv = self.pool_w_in[j].rearrange("(k p) c -> p k c", p=128)
        subs = subs_of(n)
        for cb in range(4):
            wk, w = self.wload(Wv[:, :, cb * 512:(cb + 1) * 512], KC, 512)
            g = cb
            win = 2 << g
            for oc in range(4):
                OC = cb * 4 + oc
                for si, (o, m) in enumerate(subs):
                    bk, bank = self.bank(m)

                    def mm(e, oc=oc, o=o, m=m, bank=bank, w=w):
                        for kc in range(KC):
                            r = e.matmul(bank, lhsT=w[:, kc, oc * 128:(oc + 1) * 128], rhs=self.HB[:, kc, o:o + m], start=(kc == 0), stop=(kc == KC - 1))
                        return r
                    S.add("pe", mm, reads=[wk] + [("hb", kc, si) for kc in range(KC)], writes=[bk])
                    hi = self.hh_i
                    self.hh_i ^= 1
                    hh = self.HH[hi]
                    W_ = 16 + m
                    S.add("act", lambda e, hh=hh, bank=bank, m=m: e.activation(out=hh[:, 16:16 + m], in_=bank, func=AF.Copy), reads=[bk], writes=[("hh", hi)])
                    S.add("dve", lambda e, hh=hh, OC=OC: e.tensor_copy(out=hh[:, 0:16], in_=self.CARRY[:, OC, :]), reads=[("carry", OC)], writes=[("hh", hi)])
                    S.add("dve", lambda e, hh=hh, OC=OC, m=m: e.tensor_copy(out=self.CARRY[:, OC, :], in_=hh[:, m:m + 16]), reads=[("hh", hi)], writes=[("carry", OC)])
                    if carry_only:
                        continue
                    st0, st1 = self.ST
                    S.add("dve", lambda e, hh=hh, W_=W_: e.tensor_tensor(out=st0[:, 1:W_], in0=hh[:, 1:W_], in1=hh[:, 0:W_ - 1], op=ALU.add), reads=[("hh", hi)], writes=[("st", 0)])
                    cur, curk = st0, ("st", 0)
                    if g >= 1:
                        S.add("dve", lambda e, W_=W_: e.tensor_tensor(out=st1[:, 3:W_], in0=st0[:, 3:W_], in1=st0[:, 1:W_ - 2], op=ALU.add), reads=[("st", 0)], writes=[("st", 1)])
                        cur, curk = st1, ("st", 1)
                    if g >= 2:
                        S.add("dve", lambda e, W_=W_: e.tensor_tensor(out=st0[:, 7:W_], in0=st1[:, 7:W_], in1=st1[:, 3:W_ - 4], op=ALU.add), reads=[("st", 1)], writes=[("st", 0)])
                        cur, curk = st0, ("st", 0)
                    if g >= 3:
                        S.add("dve", lambda e, W_=W_: e.tensor_tensor(out=st1[:, 15:W_], in0=st0[:, 15:W_], in1=st0[:, 7:W_ - 8], op=ALU.add), reads=[("st", 0)], writes=[("st", 1)])
                        cur, curk = st1, ("st", 1)
                    dst = self.GB[:, OC, o:o + m]
                    S.add("dve", lambda e, cur=cur, hh=hh, dst=dst, m=m, win=win: e.scalar_tensor_tensor(out=dst, in0=cur[:, 16:16 + m], scalar=1.0 / win, in1=hh[:, 16:16 + m],
                                                                                                   op0=ALU.mult, op1=ALU.subtract),
                          reads=[curk, ("hh", hi)], writes=[("gb", OC, si)])
                    if fix_col is not None and o <= fix_col < o + m:
                        c0 = fix_col - o
                        S.add("dve", lambda e, cur=cur, c0=c0, g=g: e.tensor_tensor(out=self.T16, in0=cur[:, 16 + c0:32 + c0], in1=self.ICNT[:, g, :], op=ALU.mult),
                              reads=[curk, "icnt"], writes=["t16"])
                        S.add("dve", lambda e, hh=hh, c0=c0, OC=OC, o=o: e.tensor_tensor(out=self.GB[:, OC, o + c0:o + c0 + 16], in0=self.T16, in1=hh[:, 16 + c0:32 + c0], op=ALU.subtract),
                              reads=["t16", ("hh", hi)], writes=[("gb", OC, si)])
        if carry_only:
            return
        for g in range(4):
            wk, w = self.wload(self.pool_w_group[j, g].rearrange("(k p) c -> p k c", p=128), 4, 512)
            for oc in range(4):
                OC = g * 4 + oc
                for si, (o, m) in enumerate(subs):
                    bk, bank = self.bank(m)

                    def mm(e, oc=oc, o=o, m=m, bank=bank, w=w, g=g):
                        for kc in range(4):
                            r = e.matmul(bank, lhsT=w[:, kc, oc * 128:(oc + 1) * 128], rhs=self.GB[:, g * 4 + kc, o:o + m], start=(kc == 0), stop=(kc == 3))
                        return r
                    S.add("pe", mm, reads=[wk] + [("gb", g * 4 + kc, si) for kc in range(4)], writes=[bk])
                    dst = self.HB[:, OC, o:o + m]
                    S.add("act", lambda e, dst=dst, bank=bank, OC=OC: e.activation(out=dst, in_=bank, func=AF.Copy, scale=self.gain(144 + j * 16 + OC)),
                          reads=[bk, "gains"], writes=[("hb", OC, si)])
        self.linear_a(self.HB, "hb", self.pool_w_out[j], 0, D, n, self.evac_addx)

    def sgu_init(self):
        S = self.S
        for hf in range(2):
            S.add("sp", lambda e, hf=hf: e.dma_start(out=self.TMP[hf], in_=self.wsT[:, hf * 512:(hf + 1) * 512]), writes=[("tmp", hf)], dma_key=("tmpd", hf))
            for gg in range(4):
                g = hf * 4 + gg
                S.add("dve", lambda e, hf=hf, gg=gg, g=g: e.tensor_tensor(out=self.WSB[:, g * 128:(g + 1) * 128], in0=self.TMP[hf][:, gg * 128:(gg + 1) * 128],
                                                                          in1=self.TRIL, op=ALU.mult), reads=[("tmp", hf), "tril"], writes=["wsb"])
        S.add("sp", lambda e: e.dma_start(out=self.BS, in_=self.bsb), writes=["bs"], dma_key="c_bs")

    def sgu_mixer(self, layer, n):
        S = self.S
        self.norm(n, layer * 16)
        W = self.sgu_w_in[0]

        def evac_gelu(OC, si, o, m, bk, bank):
            dst = self.GB[:, OC, o:o + m]
            S.add("act", lambda e: e.activation(out=dst, in_=bank, func=AF.Gelu), reads=[bk], writes=[("gb", OC, si)])
        self.linear_a(self.HB, "hb", W, 0, D, n, evac_gelu)
        Wv = W.rearrange("(k p) c -> p k c", p=128)
        V = self.MB.rearrange("p (b c) -> p b c", b=4)
        nb = n // 128
        for g0 in range(0, nb, 4):
            tbs = list(range(g0, min(nb, g0 + 4)))
            S.add("dve", lambda e: e.memset(self.SSQ, 0.0), writes=["ssq"])
            for cb in range(4):
                wk, w = self.wload(Wv[:, :, D + cb * 512: D + (cb + 1) * 512], KC, 512)
                for ti, tb in enumerate(tbs):
                    self._sgu_v(wk, w, cb, ti, tb, V)
            for ti, tb in enumerate(tbs):
                self._sgu_sp(ti, tb, V)
        self.linear_a(self.GB, "gb", self.sgu_w_out[0], 0, D, n, self.evac_addx)

    def _sgu_v(self, wk, w, cb, ti, tb, V):
        S = self.S
        si = (tb * 128) // 512
        bk, bank = self.bank(512)

        def mm(e):
            for kc in range(KC):
                r = e.matmul(bank, lhsT=self.HB[:, kc, tb * 128:(tb + 1) * 128], rhs=w[:, kc, :], start=(kc == 0), stop=(kc == KC - 1))
            return r
        S.add("pe", mm, reads=[wk] + [("hb", kc, si) for kc in range(KC)], writes=[bk])
        t_i = self.tmp_i
        self.tmp_i ^= 1
        tmp = self.TMP[t_i]
        S.add("act", lambda e: e.activation(out=tmp, in_=bank, func=AF.Gelu), reads=[bk], writes=[("tmp", t_i)])
        sk = self.sq_i
        self.sq_i ^= 1
        col = ti * 4 + cb
        S.add("act", lambda e: e.activation(out=self.SQ[sk], in_=tmp, func=AF.Square, accum_out=self.SSQ[:, col:col + 1]),
              reads=[("tmp", t_i), "ssq"], writes=[("sq", sk), ("ssqc", col)])
        S.add("dve", lambda e: e.tensor_copy(out=V[:, ti, cb * 512:(cb + 1) * 512], in_=tmp), reads=[("tmp", t_i)], writes=[("v", ti, cb)])

    def _sgu_sp(self, ti, tb, V):
        S = self.S
        si = (tb * 128) // 512
        rv = self.RV[:, ti:ti + 1]
        S.add("dve", lambda e: e.reduce_sum(out=rv, in_=self.SSQ[:, ti * 4:(ti + 1) * 4], axis=AX.X),
              reads=[("ssqc", ti * 4 + c) for c in range(4)] + ["ssq"], writes=[("rv", ti)])
        S.add("act", lambda e: e.activation(out=rv, in_=rv, func=AF.Sqrt, scale=1.0 / D, bias=EPS), reads=[("rv", ti)], writes=[("rv", ti)])
        S.add("dve", lambda e: e.reciprocal(out=rv, in_=rv), reads=[("rv", ti)], writes=[("rv", ti)])
        wsp = self.WSP[0]
        S.add("dve", lambda e: e.tensor_scalar(out=wsp, in0=self.WSB, scalar1=rv, scalar2=None, op0=ALU.mult), reads=["wsb", ("rv", ti)], writes=["wsp"])
        for q4 in range(4):
            bk, bank = self.bank(512)

            def mm(e, q4=q4, bank=bank):
                for x in range(4):
                    cc = q4 * 4 + x
                    g = cc // 2
                    r = e.matmul(bank[:, x * 128:(x + 1) * 128], lhsT=V[:, ti, cc * 128:(cc + 1) * 128], rhs=wsp[:, g * 128:(g + 1) * 128], start=True, stop=True)
                return r
            S.add("pe", mm, reads=["wsp"] + [("v", ti, c) for c in range(4)], writes=[bk])
            for x in range(4):
                cc = q4 * 4 + x
                g = cc // 2
                sk = self.sq_i
                self.sq_i ^= 1
                t32 = self.SQ[sk][:, 0:128]
                S.add("dve", lambda e, x=x, cc=cc, g=g, t32=t32, bank=bank: e.scalar_tensor_tensor(out=t32, in0=bank[:, x * 128:(x + 1) * 128], scalar=self.gain(176 + cc),
                                                                                                   in1=self.BS[:, g * 128:(g + 1) * 128], op0=ALU.mult, op1=ALU.add),
                      reads=[bk, "bs", "gains"], writes=[("sq", sk)])
                dst = self.GB[:, cc, tb * 128:(tb + 1) * 128]
                S.add("dve", lambda e, t32=t32, dst=dst: e.tensor_tensor(out=dst, in0=t32, in1=dst, op=ALU.mult), reads=[("sq", sk), ("gb", cc, si)], writes=[("gb", cc, si)])

    def attn_mixer(self, layer, b0, nb, kv_only):
        S = self.S
        n, t0 = nb * 128, b0 * 128
        self.norm(n, layer * 16)
        W = self.attn_w_qkv[0]
        Wv = W.rearrange("(k p) c -> p k c", p=128)

        def proj_fm(col0, DST):
            def evac(OC, si, o, m, bk, bank):
                slot = OC % 16
                dst = self.GB[:, slot, o:o + m]
                if OC % 2 == 0:
                    S.add("act", lambda e: e.activation(out=dst, in_=bank, func=AF.Copy), reads=[bk], writes=[("gb", slot, si)])
                else:
                    S.add("dve", lambda e: e.tensor_copy(out=dst, in_=bank), reads=[bk], writes=[("gb", slot, si)])
                if OC % 4 == 3 and si == len(subs_of(n)) - 1:
                    c0 = OC - 3
                    s0 = c0 % 16
                    S.add("sp", lambda e: e.dma_start(out=DST[c0:c0 + 4].rearrange("c p t -> p c t")[:, :, t0:t0 + n], in_=self.GB[:, s0:s0 + 4, 0:n]),
                          reads=[("gb", s0 + i, s) for i in range(4) for s in range(2)], dma_key=("gbst", s0 // 4))
            self.linear_a(self.HB, "hb", W, col0, 3072, n, evac)
        proj_fm(3072, self.KTD)
        if not kv_only:
            proj_fm(0, self.QTD)
        VST = self.MB[:, 0:nb * 512].rearrange("p (b c) -> p b c", b=nb)
        for cb in range(6):
            wk, w = self.wload(Wv[:, :, 6144 + cb * 512: 6144 + (cb + 1) * 512], KC, 512)
            for ti in range(nb):
                self._attn_v(wk, w, ti, VST)
            for h4 in range(4):
                S.add("sp", lambda e, cb=cb, h4=h4: e.dma_start(out=self.VD[cb * 4 + h4, b0:b0 + nb].rearrange("b p d -> p b d"),
                                                                in_=VST[:, :, h4 * 128:(h4 + 1) * 128]),
                      reads=[("vst", ti) for ti in range(nb)], dma_key="vst")
        if kv_only:
            return
        S.barrier()
        self.carve_attn()
        S.add("sp", lambda e: e.dma_start(out=self.MASK, in_=self.cmask), writes=["mask"], dma_key="mask")
        S.add("sp", lambda e: e.dma_start(out=self.VALT[:, 0:nb, :], in_=self.valt.rearrange("p (b j) -> p b j", j=NKB)[:, b0:b0 + nb, :]), writes=["valt"], dma_key="valt")
        iters = [(hd, hd % 2, qb, (hd * nb + qb) % 2) for hd in range(8) for qb in range(nb)]

        def head_loads(hd):
            sl = hd % 2
            QTv = self.QTD.rearrange("(g h) p t -> h p g t", h=8)
            S.add("sp", lambda e: e.dma_start(out=self.QH[sl][:, :, 0:n], in_=QTv[hd][:, :, t0:t0 + n]), writes=[("qh", sl)], dma_key=("qh", sl))
            for g in range(3):
                wb = GW[g] // 128
                S.add("sp", lambda e, g=g: e.dma_start(out=self.KH[sl][g][:, 0:n + GW[g]], in_=self.KTD[g * 8 + hd][:, t0 - GW[g]:t0 + n]),
                      writes=[("kh", sl, g)], dma_key=("kh", sl, g))
                S.add("sp", lambda e, g=g, wb=wb: e.dma_start(out=self.VH[sl][g][:, 0:nb + wb, :], in_=self.VD[g * 8 + hd, b0 - wb:b0 + nb].rearrange("b p d -> p b d")),
                      writes=[("vh", sl, g)], dma_key=("vh", sl, g))
        head_loads(0)
        self._a_scores(*iters[0])
        self._a_soft(*iters[0])
        for i, itx in enumerate(iters):
            nxt = iters[i + 1] if i + 1 < len(iters) else None
            if nxt is not None:
                if nxt[2] == 0:
                    head_loads(nxt[0])
                self._a_scores(*nxt)
            self._b_tr(*itx)
            if nxt is not None:
                self._a_soft(*nxt)
            self._b_pv(*itx)
        S.barrier()
        self.carve_linear()
        Wo = self.attn_w_out[0].rearrange("(k p) c -> p k c", p=128)
        wk0, w0 = self.wload(Wo[:, 0:4, :], 4, D)
        wk1, w1 = self.wload(Wo[:, 4:8, :], 4, D)
        for si, (o, m) in enumerate(subs_of(n)):
            for oc in range(KC):
                bk, bank = self.bank(m)

                def mm(e, oc=oc, o=o, m=m, bank=bank):
                    for kc in range(8):
                        w = w0 if kc < 4 else w1
                        r = e.matmul(bank, lhsT=w[:, kc % 4, oc * 128:(oc + 1) * 128], rhs=self.GB[:, kc, o:o + m], start=(kc == 0), stop=(kc == 7))
                    return r
                S.add("pe", mm, reads=[wk0, wk1] + [("gb", kc, si) for kc in range(8)], writes=[bk])
                self.evac_addx(oc, si, o, m, bk, bank)

    def _attn_v(self, wk, w, ti, VST):
        S = self.S
        si = (ti * 128) // 512
        bk, bank = self.bank(512)

        def mm(e):
            for kc in range(KC):
                r = e.matmul(bank, lhsT=self.HB[:, kc, ti * 128:(ti + 1) * 128], rhs=w[:, kc, :], start=(kc == 0), stop=(kc == KC - 1))
            return r
        S.add("pe", mm, reads=[wk] + [("hb", kc, si) for kc in range(KC)], writes=[bk])
        if ti % 2 == 0:
            S.add("act", lambda e: e.activation(out=VST[:, ti, :], in_=bank, func=AF.Copy), reads=[bk], writes=[("vst", ti)])
        else:
            S.add("dve", lambda e: e.tensor_copy(out=VST[:, ti, :], in_=bank), reads=[bk], writes=[("vst", ti)])

    def _a_scores(self, hd, sl, qb, ss):
        S = self.S
        PS = self.PS
        QH, KH = self.QH[sl], self.KH[sl]
        q0 = qb * 128

        def scores(e):
            for g, kc0, ncol, pc0 in [(2, 0, 512, 0), (2, 512, 512, 512), (2, 1024, 512, 1024), (2, 1536, 512, 1536), (2, 2048, 128, 2048),
                                      (0, 0, 256, 2176), (1, 512, 128, 2432), (1, 0, 512, 2560)]:
                r = e.matmul(PS[:, pc0:pc0 + ncol], lhsT=QH[:, g, q0:q0 + 128], rhs=KH[g][:, q0 + kc0:q0 + kc0 + ncol], start=True, stop=True)
            return r
        S.add("pe", scores, reads=[("qh", sl)] + [("kh", sl, g) for g in range(3)], writes=[("ps", b) for b in range(6)])

    def _a_soft(self, hd, sl, qb, ss):
        S = self.S
        PS = self.PS
        SSb, PP = self.SS[0], self.PP[ss]
        S.add("dve", lambda e: e.tensor_tensor(out=SSb, in0=PS[:, 0:NSC], in1=self.MASK, op=ALU.add), reads=[("ps", b) for b in range(6)] + ["mask"], writes=[("ss", 0)])
        c = ss * 2
        mx = self.SMALL[:, c:c + 1]
        nbv = self.SMALL[:, c + 1:c + 2]
        S.add("dve", lambda e: e.reduce_max(out=mx, in_=SSb, axis=AX.X), reads=[("ss", 0)], writes=[("mx", ss)])
        S.add("dve", lambda e: e.tensor_scalar(out=nbv, in0=mx, scalar1=-SCALE, scalar2=None, op0=ALU.mult), reads=[("mx", ss)], writes=[("nb", ss)])
        S.add("act", lambda e: e.activation(out=PP, in_=SSb, func=AF.Exp, scale=SCALE, bias=nbv), reads=[("ss", 0), ("nb", ss)], writes=[("pp", ss)])

    def _b_tr(self, hd, sl, qb, ss):
        S = self.S
        PS = self.PS
        PP, PT = self.PP[ss], self.PT[ss]
        b6 = PS[:, 6 * 512:7 * 512].bitcast(BF16)
        for r_ in range(3):
            def tr(e, r_=r_):
                for x in range(8):
                    jx = r_ * 8 + x
                    r = e.transpose(b6[:, x * 128:(x + 1) * 128], PP[:, jx * 128:(jx + 1) * 128], self.IDB)
                return r
            S.add("pe", tr, reads=[("pp", ss), "idb"], writes=[("ps", 6)])
            vb = self.VALT[:, qb, r_ * 8:(r_ + 1) * 8].unsqueeze(2).to_broadcast([128, 8, 128])
            S.add("dve", lambda e, r_=r_, vb=vb: e.tensor_tensor(out=PT[:, r_ * 8:(r_ + 1) * 8, :], in0=b6.rearrange("p (j q) -> p j q", q=128), in1=vb, op=ALU.mult),
                  reads=[("ps", 6), "valt"], writes=[("pt", ss, r_)])

    def _b_pv(self, hd, sl, qb, ss):
        S = self.S
        PS = self.PS
        VH = self.VH[sl]
        q0 = qb * 128
        PT = self.PT[ss]
        O = PS[:, 7 * 512:7 * 512 + 128]
        DN = PS[:, 7 * 512 + 256:7 * 512 + 384]

        def pv(e):
            for jx, (g, kb) in enumerate(SC_ORDER):
                e.matmul(O, lhsT=VH[g][:, qb + kb, :], rhs=PT[:, jx, :], start=(jx == 0), stop=(jx == NKB - 1))
            for jx in range(NKB):
                r = e.matmul(DN, lhsT=self.ONESB, rhs=PT[:, jx, :], start=(jx == 0), stop=(jx == NKB - 1))
            return r
        S.add("pe", pv, reads=[("pt", ss, r_) for r_ in range(3)] + [("vh", sl, g) for g in range(3)] + ["onesb"], writes=[("ps", 7)])
        S.add("dve", lambda e: e.tensor_scalar(out=self.RD, in0=DN, scalar1=1e-30, scalar2=None, op0=ALU.add), reads=[("ps", 7)], writes=["rd"])
        S.add("dve", lambda e: e.reciprocal(out=self.RD, in_=self.RD), reads=["rd"], writes=["rd"])
        dst = self.GB[:, hd, q0:q0 + 128]
        si = q0 // 512
        S.add("dve", lambda e: e.tensor_tensor(out=dst, in0=O, in1=self.RD, op=ALU.mult), reads=[("ps", 7), "rd"], writes=[("gb", hd, si)])

    def build(self):
        S = self.S
        self.init()
        if self.nlayers > 1:
            self.sgu_init()
        L = self.nlayers
        last = L - 1
        for layer in range(L):
            kind = layer % 3
            j = layer // 3
            src = self.xT if layer == 0 else self.XR
            if layer == 0:
                tl = [(0, 1, True)] + [(b, n, False) for b, n in tiles_of(1, NB, 5)]
            elif layer == 1:
                tl = [(b, n, False) for b, n in tiles_of(1, NB, 5)]
            elif layer == 2:
                tl = [(b, n, True) for b, n in tiles_of(1, 17, 2)] + [(b, n, False) for b, n in tiles_of(17, NB, 3)]
            else:
                tl = [(17, 1, True)] + [(b, n, False) for b, n in tiles_of(18, NB, 2)]
            for (b0, nb, partial) in tl:
                t0, n = b0 * 128, nb * 128
                self.load_x(src, t0, n)
                if kind == 0:
                    fix = None
                    if b0 <= 18 < b0 + nb:
                        fix = (18 - b0) * 128
                    self.pool_mixer(j, layer, n, partial, fix)
                elif kind == 1:
                    self.sgu_mixer(layer, n)
                else:
                    self.attn_mixer(layer, b0, nb, partial)
                if partial:
                    continue
                self.mlp(layer, n)
                if layer == last and not self.debug_x:
                    self.norm(n, 128, inplace=True)
                    self.store_x(self.OUT, t0 - HALO, n)
                elif layer == last:
                    self.store_x(self.OUT, t0, n)
                else:
                    self.store_x(self.XR, t0, n)
        S.emit()
        return self.nc


def _feat(v):
    v = np.asarray(v, np.float32).reshape(-1, KC, 128)
    return np.ascontiguousarray(v.transpose(2, 0, 1)).reshape(128, -1)


def prep_shared(inp, nlayers=4):
    sh = {}
    g = np.concatenate([_feat(inp["norm_mix"]), _feat(inp["norm_mlp"]), _feat(inp["norm_final"]),
                        _feat(inp["pool_scale"]), _feat(inp["sgu_v_norm"])], axis=1)
    sh["gains"] = np.ascontiguousarray(g, np.float32)
    sh["idb"] = np.eye(128).astype(ml_dtypes.bfloat16)
    for k in ["pool_w_in", "pool_w_group", "pool_w_out", "mlp_w_up", "mlp_w_down"]:
        sh[k] = np.ascontiguousarray(inp[k], np.float32)
    if nlayers > 1:
        sh["sgu_w_in"] = np.ascontiguousarray(inp["sgu_w_in"], np.float32)
        sh["sgu_w_out"] = np.ascontiguousarray(inp["sgu_w_out"], np.float32)
        ws = np.asarray(inp["sgu_w_s"], np.float32)[0]
        sh["wsT"] = np.ascontiguousarray(ws.transpose(2, 0, 1)).reshape(128, 1024)
        bs = np.asarray(inp["sgu_b_s"], np.float32)[0]
        sh["bsb"] = np.ascontiguousarray(np.broadcast_to(bs.reshape(1, 1024), (128, 1024)))
    if nlayers > 2:
        sh["attn_w_qkv"] = np.ascontiguousarray(inp["attn_w_qkv"], np.float32)
        sh["attn_w_out"] = np.ascontiguousarray(inp["attn_w_out"], np.float32)
        cm = np.full((128, NSC), -1e30, np.float32)
        q = np.arange(128)[:, None]
        kk = np.arange(128)[None, :]
        for jx, (g_, kb) in enumerate(SC_ORDER):
            delta = q + GW[g_] - (kb * 128 + kk)
            ok = (delta >= 0) & (delta <= GW[g_]) & (delta % GD[g_] == 0)
            cm[:, jx * 128:(jx + 1) * 128] = np.where(ok, 0.0, -1e30)
        sh["cmask"] = cm
    return sh


def prep_core(inp, c, nlayers=4):
    pc = {}
    x = np.asarray(inp["x"], np.float32)[0]
    S0 = c * OWN
    xp = np.zeros((NT, D), np.float32)
    lo = S0 - HALO
    a = max(lo, 0)
    xp[a - lo:] = x[a:S0 + OWN]
    pc["xT"] = np.ascontiguousarray(xp.T).reshape(KC, 128, NT)
    icnt = np.zeros((4, 16), np.float32)
    for g_ in range(4):
        w = 2 << g_
        for t in range(16):
            icnt[g_, t] = 1.0 / min(t + 1, w) if c == 0 else 1.0 / w
    tril = (np.arange(128)[:, None] <= np.arange(128)[None, :]).astype(np.float32)
    pc["consts"] = np.ascontiguousarray(np.concatenate(
        [np.ones((128, 128), np.float32), np.broadcast_to(icnt.reshape(1, 64), (128, 64)), tril], axis=1))
    if nlayers > 2:
        valid_blk = np.array([1.0 if (lo + b * 128) >= 0 else 0.0 for b in range(NB)], np.float32)
        vt = np.zeros((NB, NKB), np.float32)
        for b in range(NB):
            for jx, (g_, kb) in enumerate(SC_ORDER):
                kblk = b - GW[g_] // 128 + kb
                vt[b, jx] = valid_blk[kblk] if kblk >= 0 else 0.0
        pc["valt"] = np.ascontiguousarray(np.broadcast_to(vt.reshape(1, -1), (128, NB * NKB)))
    return pc


def kernel(**inputs):
    b = Builder(4, False)
    nc = b.build()
    sh = prep_shared(inputs, 4)
    in_maps = []
    for c in range(NCORES):
        m = dict(sh)
        m.update(prep_core(inputs, c, 4))
        in_maps.append(m)
    res = run_bass_kernel_spmd(nc, in_maps, core_ids=list(range(NCORES)))
    outs = []
    for c in range(NCORES):
        o = np.asarray(res.results[c]["OUT"], np.float32).reshape(D, OWN)
        outs.append(o.T)
    return np.ascontiguousarray(np.concatenate(outs, axis=0)).reshape(1, NCORES * OWN, D).astype(np.float32)
```

```python
import contextlib
import numpy as np
import ml_dtypes
import concourse.bass as bass
import concourse.mybir as mybir
from concourse.bass_utils import run_bass_kernel_spmd

F32 = mybir.dt.float32
BF16 = mybir.dt.bfloat16
AF = mybir.ActivationFunctionType
ALU = mybir.AluOpType
AX = mybir.AxisListType

ENGS = ["pe", "act", "dve", "pool", "sp"]


class Sched:
    def __init__(self, nc):
        self.nc = nc
        self.ops = []
        self.eng_ops = {e: [] for e in ENGS}
        self.lastw = {}
        self.readers = {}
        self.dma_count = {}
        self.last_dma = {}
        self.stack = contextlib.ExitStack()

    def sb(self, name, shape, dt):
        return self.stack.enter_context(self.nc.sbuf_tensor(name, shape, dt))

    def ps(self, name, shape, dt):
        return self.stack.enter_context(self.nc.psum_tensor(name, shape, dt))

    def add(self, eng, fn, reads=(), writes=(), dma_key=None):
        oid = len(self.ops)
        deps = set()
        rk = list(reads)
        wk = list(writes)
        if dma_key is not None:
            wk.append(("dmaq", dma_key))
        for k in rk:
            if k in self.lastw:
                deps.add(self.lastw[k])
        for k in wk:
            if k in self.lastw:
                deps.add(self.lastw[k])
            for r in self.readers.get(k, ()):
                deps.add(r)
        for k in rk:
            self.readers.setdefault(k, []).append(oid)
        for k in wk:
            self.lastw[k] = oid
            self.readers[k] = []
        op = dict(id=oid, eng=eng, fn=fn, deps=deps, dma_key=dma_key, ms=None, need=False)
        if dma_key is not None:
            c = self.dma_count.get(dma_key, 0) + 1
            self.dma_count[dma_key] = c
            op["dma_cnt"] = c
            self.last_dma[dma_key] = oid
        self.ops.append(op)
        self.eng_ops[eng].append(op)
        return oid

    def barrier(self):
        last = set()
        for e in ENGS:
            for op in reversed(self.eng_ops[e]):
                if op["fn"] is not None and op["dma_key"] is None:
                    last.add(op["id"])
                    break
        for k, oid in self.last_dma.items():
            last.add(oid)
        for e in ENGS:
            oid = len(self.ops)
            op = dict(id=oid, eng=e, fn=None, deps=set(last), dma_key=None, ms=None, need=False)
            self.ops.append(op)
            self.eng_ops[e].append(op)
        self.lastw = {}
        self.readers = {}

    def emit(self, final_wait=True):
        nc = self.nc
        ops = self.ops
        if final_wait:
            self.barrier()
        for op in ops:
            for d in op["deps"]:
                dop = ops[d]
                if dop["dma_key"] is None:
                    dop["need"] = True
        cnt = {e: 0 for e in ENGS}
        for e in ENGS:
            for op in self.eng_ops[e]:
                if op["need"]:
                    cnt[e] += 1
                    op["ms"] = cnt[e]
        st = self.stack
        sem_e = {e: st.enter_context(nc.semaphore("se_" + e)) for e in ENGS}
        sem_d = {}
        for i, k in enumerate(self.dma_count):
            sem_d[k] = st.enter_context(nc.semaphore("sd_%d" % i))
        self.n_sems = len(sem_e) + len(sem_d)
        self.n_waits = 0
        block = st.enter_context(nc.Block())

        def body(ename):
            def f(eng):
                waited = {}
                for op in self.eng_ops[ename]:
                    for d in sorted(op["deps"]):
                        dop = ops[d]
                        if dop["dma_key"] is not None:
                            sem, val = sem_d[dop["dma_key"]], 16 * dop["dma_cnt"]
                        else:
                            if dop["eng"] == ename and ename == "pe":
                                continue
                            sem, val = sem_e[dop["eng"]], dop["ms"]
                        key = id(sem)
                        if waited.get(key, 0) >= val:
                            continue
                        waited[key] = val
                        eng.wait_ge(sem, val)
                        self.n_waits += 1
                    if op["fn"] is None:
                        continue
                    ins = op["fn"](eng)
                    if op["dma_key"] is not None:
                        ins.then_inc(sem_d[op["dma_key"]], 16)
                    elif op["ms"] is not None:
                        ins.then_inc(sem_e[ename], 1)
            return f

        block.tensor(body("pe"))
        block.scalar(body("act"))
        block.vector(body("dve"))
        block.gpsimd(body("pool"))
        block.sync(body("sp"))
        st.close()


NCORES = 8
D = 2048
KC = 16
OWN = 2048
NB = 34
NT = NB * 128
HALO = NT - OWN
TM = 896
EPS = 1e-6
SCALE = 128.0 ** -0.5
GW = (128, 512, 2048)
GD = (1, 4, 16)
SC_ORDER = [(2, k) for k in range(17)] + [(0, 0), (0, 1), (1, 4)] + [(1, k) for k in range(4)]
NKB = len(SC_ORDER)
NSC = NKB * 128


def tiles_of(b0, b1, first):
    out = [(b0, first)] if first else []
    b = b0 + first
    while b < b1:
        n = min(7, b1 - b)
        out.append((b, n))
        b += n
    return out


def subs_of(n):
    out, o = [], 0
    while o < n:
        m = min(512, n - o)
        out.append((o, m))
        o += m
    return out


class Builder:
    def __init__(self, nlayers=4, debug_x=False):
        self.nlayers = nlayers
        self.debug_x = debug_x
        nc = self.nc = bass.Bass("TRN2", target_bir_lowering=False)
        S = self.S = Sched(nc)
        di = lambda name, shape, dt=F32: nc.dram_tensor(name, shape, dt, kind="ExternalInput").ap()
        self.xT = di("xT", [KC, 128, NT])
        self.gains = di("gains", [128, 192])
        self.consts = di("consts", [128, 128 + 64 + 128])
        self.idb = di("idb", [128, 128], BF16)
        self.pool_w_in = di("pool_w_in", [2, D, D])
        self.pool_w_group = di("pool_w_group", [2, 4, 512, 512])
        self.pool_w_out = di("pool_w_out", [2, D, D])
        self.mlp_w_up = di("mlp_w_up", [4, D, 4 * D])
        self.mlp_w_down = di("mlp_w_down", [4, 4 * D, D])
        if nlayers > 1:
            self.sgu_w_in = di("sgu_w_in", [1, D, 2 * D])
            self.sgu_w_out = di("sgu_w_out", [1, D, D])
            self.wsT = di("wsT", [128, 8 * 128])
            self.bsb = di("bsb", [128, 8 * 128])
        if nlayers > 2:
            self.attn_w_qkv = di("attn_w_qkv", [1, D, 9216])
            self.attn_w_out = di("attn_w_out", [1, 1024, D])
            self.cmask = di("cmask", [128, NSC])
            self.valt = di("valt", [128, NB * NKB])
            self.QTD = nc.dram_tensor("QTD", [24, 128, NT], BF16, kind="Internal").ap()
            self.KTD = nc.dram_tensor("KTD", [24, 128, NT], BF16, kind="Internal").ap()
            self.VD = nc.dram_tensor("VD", [24, NB, 128, 128], BF16, kind="Internal").ap()
        self.XR = nc.dram_tensor("XR", [KC, 128, NT], F32, kind="Internal").ap()
        if debug_x:
            self.OUT = nc.dram_tensor("OUT", [KC, 128, NT], F32, kind="ExternalOutput").ap()
        else:
            self.OUT = nc.dram_tensor("OUT", [KC, 128, OWN], F32, kind="ExternalOutput").ap()

        AW = 53200
        ar = self.ar = S.sb("arena", [128, AW], F32)
        self.off = 0

        def carve(words):
            o = self.off
            self.off += words
            assert self.off <= AW, self.off
            return ar[:, o:o + words]
        self.ONES = carve(128)
        self.ICNT = carve(64).rearrange("p (g c) -> p g c", g=4)
        self.TRIL = carve(128)
        self.GAINS = carve(192)
        self.IDB = carve(64).bitcast(BF16)
        self.ONESB = carve(64).bitcast(BF16)
        self.SQ = [carve(512) for _ in range(2)]
        self.RS = carve(512)
        self.HH = [carve(528) for _ in range(2)]
        self.ST = [carve(528) for _ in range(2)]
        self.CARRY = carve(256).rearrange("p (k c) -> p k c", k=16)
        self.T16 = carve(16)
        self.TMP = [carve(512) for _ in range(2)]
        self.SMALL = carve(64)
        self.XA = carve(KC * TM).rearrange("p (k t) -> p k t", k=KC)
        self.GBw = carve(KC * TM // 2)
        self.GB = self.GBw.bitcast(BF16).rearrange("p (k t) -> p k t", k=KC)
        self.AO = self.GBw[:, 0:8 * TM // 2].bitcast(BF16).rearrange("p (k t) -> p k t", k=8)
        self.phase_base = self.off
        self.PS = S.ps("ps", [128, 4096], F32)
        self.bank_i = 0
        self.wr_i = 0
        self.sq_i = 0
        self.tmp_i = 0
        self.hh_i = 0
        self.carve = carve
        self.carve_linear()

    def carve_linear(self):
        self.off = self.phase_base
        c = self.carve
        self.HB = c(KC * TM // 2).bitcast(BF16).rearrange("p (k t) -> p k t", k=KC)
        self.WR = [c(4096).bitcast(BF16) for _ in range(3)]
        self.MB = c(4096).bitcast(BF16)
        self.WSB = c(512).bitcast(BF16)
        self.WSP = [c(512).bitcast(BF16) for _ in range(1)]
        self.BS = c(1024)
        self.SSQ = c(32)
        self.RV = c(8)

    def carve_attn(self):
        self.off = self.phase_base
        c = self.carve
        self.QH = [c(3 * TM // 2).bitcast(BF16).rearrange("p (g t) -> p g t", g=3) for _ in range(2)]
        klen = [TM + w for w in GW]
        self.KH = [[c(klen[g] // 2).bitcast(BF16) for g in range(3)] for _ in range(2)]
        self.VH = [[c((7 + GW[g] // 128) * 128 // 2).bitcast(BF16).rearrange("p (b d) -> p b d", d=128) for g in range(3)] for _ in range(2)]
        self.MASK = c(NSC)
        self.VALT = c(7 * NKB).rearrange("p (b j) -> p b j", j=NKB)
        self.SS = [c(NSC) for _ in range(1)]
        self.PP = [c(NSC // 2).bitcast(BF16) for _ in range(2)]
        self.PT = [c(NSC // 2).bitcast(BF16).rearrange("p (j q) -> p j q", q=128) for _ in range(2)]
        self.RD = c(128)

    def bank(self, n=512):
        b = self.bank_i
        self.bank_i = (b + 1) % 8
        return ("ps", b), self.PS[:, b * 512:b * 512 + n]

    def wslot(self):
        s = self.wr_i
        self.wr_i = (s + 1) % 3
        return ("wr", s), self.WR[s]

    def wload(self, src, a, b):
        key, slot = self.wslot()
        dst = slot[:, 0:a * b].rearrange("p (a b) -> p a b", a=a)
        self.S.add("pool", lambda e: e.dma_start(out=dst, in_=src), writes=[key], dma_key=key)
        return key, dst

    def gain(self, col):
        return self.GAINS[:, col:col + 1]

    def init(self):
        S = self.S
        S.add("sp", lambda e: e.dma_start(out=self.ONES, in_=self.consts[:, 0:128]), writes=["ones"], dma_key="c_ones")
        S.add("sp", lambda e: e.dma_start(out=self.ICNT, in_=self.consts[:, 128:192].rearrange("p (g c) -> p g c", g=4)), writes=["icnt"], dma_key="c_icnt")
        S.add("sp", lambda e: e.dma_start(out=self.TRIL, in_=self.consts[:, 192:320]), writes=["tril"], dma_key="c_tril")
        S.add("sp", lambda e: e.dma_start(out=self.GAINS, in_=self.gains), writes=["gains"], dma_key="c_gains")
        S.add("sp", lambda e: e.dma_start(out=self.IDB, in_=self.idb), writes=["idb"], dma_key="c_idb")
        S.add("dve", lambda e: e.tensor_copy(out=self.ONESB, in_=self.ONES), reads=["ones"], writes=["onesb"])
        S.add("dve", lambda e: e.memset(self.CARRY, 0.0), writes=[("carry", k) for k in range(KC)])

    def load_x(self, src, t0, n):
        S = self.S
        for si, (o, m) in enumerate(subs_of(n)):
            S.add("sp", lambda e, o=o, m=m: e.dma_start(out=self.XA[:, :, o:o + m], in_=src.rearrange("k p t -> p k t")[:, :, t0 + o:t0 + o + m]),
                  writes=[("xa", k, si) for k in range(KC)], dma_key=("xa", si))

    def store_x(self, dst, t0, n, c0=0):
        S = self.S
        for si, (o, m) in enumerate(subs_of(n)):
            S.add("sp", lambda e, o=o, m=m: e.dma_start(out=dst.rearrange("k p t -> p k t")[:, :, t0 + o:t0 + o + m], in_=self.XA[:, :, c0 + o:c0 + o + m]),
                  reads=[("xa", k, si) for k in range(KC)], dma_key=("xa", si))

    def norm(self, n, gcol, inplace=False):
        for si, (o, m) in enumerate(subs_of(n)):
            self._norm_sub(si, o, m, gcol, inplace)

    def _norm_sub(self, si, o, m, gcol, inplace):
        S = self.S
        bk, bank = self.bank(m)
        for kc in range(KC):
            sk = self.sq_i
            self.sq_i ^= 1
            sq = self.SQ[sk].bitcast(BF16)[:, 0:m]
            S.add("act", lambda e, kc=kc, sq=sq: e.activation(out=sq, in_=self.XA[:, kc, o:o + m], func=AF.Square),
                  reads=[("xa", kc, si)], writes=[("sq", sk)])
            S.add("pe", lambda e, kc=kc, sq=sq: e.matmul(bank, lhsT=self.ONESB, rhs=sq, start=(kc == 0), stop=(kc == KC - 1)),
                  reads=[("sq", sk), "onesb"], writes=[bk])
        rs = self.RS[:, 0:m]
        S.add("act", lambda e: e.activation(out=rs, in_=bank, func=AF.Sqrt, scale=1.0 / D, bias=EPS), reads=[bk], writes=["rs"])
        S.add("dve", lambda e: e.reciprocal(out=rs, in_=rs), reads=["rs"], writes=["rs"])
        for kc in range(KC):
            if inplace:
                dst, wk = self.XA[:, kc, o:o + m], ("xa", kc, si)
            else:
                dst, wk = self.HB[:, kc, o:o + m], ("hb", kc, si)
            S.add("dve", lambda e, kc=kc, dst=dst: e.scalar_tensor_tensor(out=dst, in0=self.XA[:, kc, o:o + m], scalar=self.gain(gcol + kc),
                                                                         in1=rs, op0=ALU.mult, op1=ALU.mult),
                  reads=[("xa", kc, si), "rs", "gains"], writes=[wk])

    def linear_a(self, inbuf, inkey, W, col0, ncols, n, evac):
        S = self.S
        Wv = W.rearrange("(k p) c -> p k c", p=128)
        for cb in range(ncols // 512):
            wk, w = self.wload(Wv[:, :, col0 + cb * 512: col0 + cb * 512 + 512], KC, 512)
            for si, (o, m) in enumerate(subs_of(n)):
                for oc in range(4):
                    bk, bank = self.bank(m)

                    def mm(e, oc=oc, o=o, m=m, bank=bank, w=w):
                        for kc in range(KC):
                            r = e.matmul(bank, lhsT=w[:, kc, oc * 128:(oc + 1) * 128], rhs=inbuf[:, kc, o:o + m], start=(kc == 0), stop=(kc == KC - 1))
                        return r
                    S.add("pe", mm, reads=[wk] + [(inkey, kc, si) for kc in range(KC)], writes=[bk])
                    evac(cb * 4 + oc, si, o, m, bk, bank)

    def evac_addx(self, OC, si, o, m, bk, bank):
        dst = self.XA[:, OC, o:o + m]
        self.S.add("dve", lambda e: e.tensor_tensor(out=dst, in0=dst, in1=bank, op=ALU.add), reads=[bk, ("xa", OC, si)], writes=[("xa", OC, si)])

    def mlp(self, layer, n):
        S = self.S
        self.norm(n, 64 + layer * 16)
        Wu = self.mlp_w_up[layer].rearrange("(k p) c -> p k c", p=128)
        Wd = self.mlp_w_down[layer].rearrange("(k p) c -> p k c", p=128)
        HID = [self.MB[:, s * 4 * TM:(s + 1) * 4 * TM].rearrange("p (k t) -> p k t", k=4) for s in range(2)]
        subs = subs_of(n)

        def up(j):
            wk, w = self.wload(Wu[:, :, j * 512:(j + 1) * 512], KC, 512)
            hs = j % 2
            for si, (o, m) in enumerate(subs):
                for oc in range(4):
                    bk, bank = self.bank(m)

                    def mm(e, oc=oc, o=o, m=m, bank=bank, w=w):
                        for kc in range(KC):
                            r = e.matmul(bank, lhsT=w[:, kc, oc * 128:(oc + 1) * 128], rhs=self.HB[:, kc, o:o + m], start=(kc == 0), stop=(kc == KC - 1))
                        return r
                    S.add("pe", mm, reads=[wk] + [("hb", kc, si) for kc in range(KC)], writes=[bk])
                    ti = self.tmp_i
                    self.tmp_i ^= 1
                    tmp = self.TMP[ti][:, 0:m]
                    S.add("act", lambda e, tmp=tmp, bank=bank: e.activation(out=tmp, in_=bank, func=AF.Relu), reads=[bk], writes=[("tmp", ti)])
                    dst = HID[hs][:, oc, o:o + m]
                    S.add("act", lambda e, tmp=tmp, dst=dst: e.activation(out=dst, in_=tmp, func=AF.Square), reads=[("tmp", ti)], writes=[("hid", hs, oc, si)])

        def down(j):
            wk, w = self.wload(Wd[:, j * 4:(j + 1) * 4, :], 4, D)
            hs = j % 2
            for si, (o, m) in enumerate(subs):
                for oc in range(KC):
                    bk, bank = self.bank(m)

                    def mm(e, oc=oc, o=o, m=m, bank=bank, w=w):
                        for kc in range(4):
                            r = e.matmul(bank, lhsT=w[:, kc, oc * 128:(oc + 1) * 128], rhs=HID[hs][:, kc, o:o + m], start=(kc == 0), stop=(kc == 3))
                        return r
                    S.add("pe", mm, reads=[wk] + [("hid", hs, kc, si) for kc in range(4)], writes=[bk])
                    self.evac_addx(oc, si, o, m, bk, bank)
        for j in range(16):
            up(j)
            if j >= 1:
                down(j - 1)
        down(15)

    def pool_mixer(self, j, layer, n, carry_only, fix_col):
        S = self.S
        self.norm(n, layer * 16)
        Wv = self.pool_w_in[j].rearrange("(k p) c -> p k c", p=128)
        subs = subs_of(n)
        for cb in range(4):
            wk, w = self.wload(Wv[:, :, cb * 512:(cb + 1) * 512], KC, 512)
            g = cb
            win = 2 << g
            for oc in range(4):
                OC = cb * 4 + oc
                for si, (o, m) in enumerate(subs):
                    bk, bank = self.bank(m)

                    def mm(e, oc=oc, o=o, m=m, bank=bank, w=w):
                        for kc in range(KC):
                            r = e.matmul(bank, lhsT=w[:, kc, oc * 128:(oc + 1) * 128], rhs=self.HB[:, kc, o:o + m], start=(kc == 0), stop=(kc == KC - 1))
                        return r
                    S.add("pe", mm, reads=[wk] + [("hb", kc, si) for kc in range(KC)], writes=[bk])
                    hi = self.hh_i
                    self.hh_i ^= 1
                    hh = self.HH[hi]
                    W_ = 16 + m
                    S.add("act", lambda e, hh=hh, bank=bank, m=m: e.activation(out=hh[:, 16:16 + m], in_=bank, func=AF.Copy), reads=[bk], writes=[("hh", hi)])
                    S.add("dve", lambda e, hh=hh, OC=OC: e.tensor_copy(out=hh[:, 0:16], in_=self.CARRY[:, OC, :]), reads=[("carry", OC)], writes=[("hh", hi)])
                    S.add("dve", lambda e, hh=hh, OC=OC, m=m: e.tensor_copy(out=self.CARRY[:, OC, :], in_=hh[:, m:m + 16]), reads=[("hh", hi)], writes=[("carry", OC)])
                    if carry_only:
                        continue
                    st0, st1 = self.ST
                    S.add("dve", lambda e, hh=hh, W_=W_: e.tensor_tensor(out=st0[:, 1:W_], in0=hh[:, 1:W_], in1=hh[:, 0:W_ - 1], op=ALU.add), reads=[("hh", hi)], writes=[("st", 0)])
                    cur, curk = st0, ("st", 0)
                    if g >= 1:
                        S.add("dve", lambda e, W_=W_: e.tensor_tensor(out=st1[:, 3:W_], in0=st0[:, 3:W_], in1=st0[:, 1:W_ - 2], op=ALU.add), reads=[("st", 0)], writes=[("st", 1)])
                        cur, curk = st1, ("st", 1)
                    if g >= 2:
                        S.add("dve", lambda e, W_=W_: e.tensor_tensor(out=st0[:, 7:W_], in0=st1[:, 7:W_], in1=st1[:, 3:W_ - 4], op=ALU.add), reads=[("st", 1)], writes=[("st", 0)])
                        cur, curk = st0, ("st", 0)
                    if g >= 3:
                        S.add("dve", lambda e, W_=W_: e.tensor_tensor(out=st1[:, 15:W_], in0=st0[:, 15:W_], in1=st0[:, 7:W_ - 8], op=ALU.add), reads=[("st", 0)], writes=[("st", 1)])
                        cur, curk = st1, ("st", 1)
                    dst = self.GB[:, OC, o:o + m]
                    S.add("dve", lambda e, cur=cur, hh=hh, dst=dst, m=m, win=win: e.scalar_tensor_tensor(out=dst, in0=cur[:, 16:16 + m], scalar=1.0 / win, in1=hh[:, 16:16 + m],
                                                                                                   op0=ALU.mult, op1=ALU.subtract),
                          reads=[curk, ("hh", hi)], writes=[("gb", OC, si)])
                    if fix_col is not None and o <= fix_col < o + m:
                        c0 = fix_col - o
                        S.add("dve", lambda e, cur=cur, c0=c0, g=g: e.tensor_tensor(out=self.T16, in0=cur[:, 16 + c0:32 + c0], in1=self.ICNT[:, g, :], op=ALU.mult),
                              reads=[curk, "icnt"], writes=["t16"])
                        S.add("dve", lambda e, hh=hh, c0=c0, OC=OC, o=o: e.tensor_tensor(out=self.GB[:, OC, o + c0:o + c0 + 16], in0=self.T16, in1=hh[:, 16 + c0:32 + c0], op=ALU.subtract),
                              reads=["t16", ("hh", hi)], writes=[("gb", OC, si)])
        if carry_only:
            return
        for g in range(4):
            wk, w = self.wload(self.pool_w_group[j, g].rearrange("(k p) c -> p k c", p=128), 4, 512)
            for oc in range(4):
                OC = g * 4 + oc
                for si, (o, m) in enumerate(subs):
                    bk, bank = self.bank(m)

                    def mm(e, oc=oc, o=o, m=m, bank=bank, w=w, g=g):
                        for kc in range(4):
                            r = e.matmul(bank, lhsT=w[:, kc, oc * 128:(oc + 1) * 128], rhs=self.GB[:, g * 4 + kc, o:o + m], start=(kc == 0), stop=(kc == 3))
                        return r
                    S.add("pe", mm, reads=[wk] + [("gb", g * 4 + kc, si) for kc in range(4)], writes=[bk])
                    dst = self.HB[:, OC, o:o + m]
                    S.add("act", lambda e, dst=dst, bank=bank, OC=OC: e.activation(out=dst, in_=bank, func=AF.Copy, scale=self.gain(144 + j * 16 + OC)),
                          reads=[bk, "gains"], writes=[("hb", OC, si)])
        self.linear_a(self.HB, "hb", self.pool_w_out[j], 0, D, n, self.evac_addx)

    def sgu_init(self):
        S = self.S
        for hf in range(2):
            S.add("sp", lambda e, hf=hf: e.dma_start(out=self.TMP[hf], in_=self.wsT[:, hf * 512:(hf + 1) * 512]), writes=[("tmp", hf)], dma_key=("tmpd", hf))
            for gg in range(4):
                g = hf * 4 + gg
                S.add("dve", lambda e, hf=hf, gg=gg, g=g: e.tensor_tensor(out=self.WSB[:, g * 128:(g + 1) * 128], in0=self.TMP[hf][:, gg * 128:(gg + 1) * 128],
                                                                          in1=self.TRIL, op=ALU.mult), reads=[("tmp", hf), "tril"], writes=["wsb"])
        S.add("sp", lambda e: e.dma_start(out=self.BS, in_=self.bsb), writes=["bs"], dma_key="c_bs")

    def sgu_mixer(self, layer, n):
        S = self.S
        self.norm(n, layer * 16)
        W = self.sgu_w_in[0]

        def evac_gelu(OC, si, o, m, bk, bank):
            dst = self.GB[:, OC, o:o + m]
            S.add("act", lambda e: e.activation(out=dst, in_=bank, func=AF.Gelu), reads=[bk], writes=[("gb", OC, si)])
        self.linear_a(self.HB, "hb", W, 0, D, n, evac_gelu)
        Wv = W.rearrange("(k p) c -> p k c", p=128)
        V = self.MB.rearrange("p (b c) -> p b c", b=4)
        nb = n // 128
        for g0 in range(0, nb, 4):
            tbs = list(range(g0, min(nb, g0 + 4)))
            S.add("dve", lambda e: e.memset(self.SSQ, 0.0), writes=["ssq"])
            for cb in range(4):
                wk, w = self.wload(Wv[:, :, D + cb * 512: D + (cb + 1) * 512], KC, 512)
                for ti, tb in enumerate(tbs):
                    self._sgu_v(wk, w, cb, ti, tb, V)
            for ti, tb in enumerate(tbs):
                self._sgu_sp(ti, tb, V)
        self.linear_a(self.GB, "gb", self.sgu_w_out[0], 0, D, n, self.evac_addx)

    def _sgu_v(self, wk, w, cb, ti, tb, V):
        S = self.S
        si = (tb * 128) // 512
        bk, bank = self.bank(512)

        def mm(e):
            for kc in range(KC):
                r = e.matmul(bank, lhsT=self.HB[:, kc, tb * 128:(tb + 1) * 128], rhs=w[:, kc, :], start=(kc == 0), stop=(kc == KC - 1))
            return r
        S.add("pe", mm, reads=[wk] + [("hb", kc, si) for kc in range(KC)], writes=[bk])
        t_i = self.tmp_i
        self.tmp_i ^= 1
        tmp = self.TMP[t_i]
        S.add("act", lambda e: e.activation(out=tmp, in_=bank, func=AF.Gelu), reads=[bk], writes=[("tmp", t_i)])
        sk = self.sq_i
        self.sq_i ^= 1
        col = ti * 4 + cb
        S.add("act", lambda e: e.activation(out=self.SQ[sk], in_=tmp, func=AF.Square, accum_out=self.SSQ[:, col:col + 1]),
              reads=[("tmp", t_i), "ssq"], writes=[("sq", sk), ("ssqc", col)])
        S.add("dve", lambda e: e.tensor_copy(out=V[:, ti, cb * 512:(cb + 1) * 512], in_=tmp), reads=[("tmp", t_i)], writes=[("v", ti, cb)])

    def _sgu_sp(self, ti, tb, V):
        S = self.S
        si = (tb * 128) // 512
        rv = self.RV[:, ti:ti + 1]
        S.add("dve", lambda e: e.reduce_sum(out=rv, in_=self.SSQ[:, ti * 4:(ti + 1) * 4], axis=AX.X),
              reads=[("ssqc", ti * 4 + c) for c in range(4)] + ["ssq"], writes=[("rv", ti)])
        S.add("act", lambda e: e.activation(out=rv, in_=rv, func=AF.Sqrt, scale=1.0 / D, bias=EPS), reads=[("rv", ti)], writes=[("rv", ti)])
        S.add("dve", lambda e: e.reciprocal(out=rv, in_=rv), reads=[("rv", ti)], writes=[("rv", ti)])
        wsp = self.WSP[0]
        S.add("dve", lambda e: e.tensor_scalar(out=wsp, in0=self.WSB, scalar1=rv, scalar2=None, op0=ALU.mult), reads=["wsb", ("rv", ti)], writes=["wsp"])
        for q4 in range(4):
            bk, bank = self.bank(512)

            def mm(e, q4=q4, bank=bank):
                for x in range(4):
                    cc = q4 * 4 + x
                    g = cc // 2
                    r = e.matmul(bank[:, x * 128:(x + 1) * 128], lhsT=V[:, ti, cc * 128:(cc + 1) * 128], rhs=wsp[:, g * 128:(g + 1) * 128], start=True, stop=True)
                return r
            S.add("pe", mm, reads=["wsp"] + [("v", ti, c) for c in range(4)], writes=[bk])
            for x in range(4):
                cc = q4 * 4 + x
                g = cc // 2
                sk = self.sq_i
                self.sq_i ^= 1
                t32 = self.SQ[sk][:, 0:128]
                S.add("dve", lambda e, x=x, cc=cc, g=g, t32=t32, bank=bank: e.scalar_tensor_tensor(out=t32, in0=bank[:, x * 128:(x + 1) * 128], scalar=self.gain(176 + cc),
                                                                                                   in1=self.BS[:, g * 128:(g + 1) * 128], op0=ALU.mult, op1=ALU.add),
                      reads=[bk, "bs", "gains"], writes=[("sq", sk)])
                dst = self.GB[:, cc, tb * 128:(tb + 1) * 128]
                S.add("dve", lambda e, t32=t32, dst=dst: e.tensor_tensor(out=dst, in0=t32, in1=dst, op=ALU.mult), reads=[("sq", sk), ("gb", cc, si)], writes=[("gb", cc, si)])

    def attn_mixer(self, layer, b0, nb, kv_only):
        S = self.S
        n, t0 = nb * 128, b0 * 128
        self.norm(n, layer * 16)
        W = self.attn_w_qkv[0]
        Wv = W.rearrange("(k p) c -> p k c", p=128)

        g_lo = 2 if (kv_only and b0 + nb <= 13) else 0

        def proj_fm(col0, DST):
            def evac(OC, si, o, m, bk, bank):
                slot = OC % 16
                dst = self.GB[:, slot, o:o + m]
                if OC % 2 == 0:
                    S.add("act", lambda e: e.activation(out=dst, in_=bank, func=AF.Copy), reads=[bk], writes=[("gb", slot, si)])
                else:
                    S.add("dve", lambda e: e.tensor_copy(out=dst, in_=bank), reads=[bk], writes=[("gb", slot, si)])
                if OC % 4 == 3 and si == len(subs_of(n)) - 1:
                    c0 = OC - 3
                    s0 = c0 % 16
                    d0 = c0 + g_lo * 8
                    S.add("sp", lambda e: e.dma_start(out=DST[d0:d0 + 4].rearrange("c p t -> p c t")[:, :, t0:t0 + n], in_=self.GB[:, s0:s0 + 4, 0:n]),
                          reads=[("gb", s0 + i, s) for i in range(4) for s in range(2)], dma_key=("gbst", s0 // 4))
            self.linear_a(self.HB, "hb", W, col0 + g_lo * 1024, 3072 - g_lo * 1024, n, evac)
        proj_fm(3072, self.KTD)
        if not kv_only:
            proj_fm(0, self.QTD)
        VST = self.MB[:, 0:nb * 512].rearrange("p (b c) -> p b c", b=nb)
        for cb in range(2 * g_lo, 6):
            wk, w = self.wload(Wv[:, :, 6144 + cb * 512: 6144 + (cb + 1) * 512], KC, 512)
            for ti in range(nb):
                self._attn_v(wk, w, ti, VST)
            for h4 in range(4):
                S.add("sp", lambda e, cb=cb, h4=h4: e.dma_start(out=self.VD[cb * 4 + h4, b0:b0 + nb].rearrange("b p d -> p b d"),
                                                                in_=VST[:, :, h4 * 128:(h4 + 1) * 128]),
                      reads=[("vst", ti) for ti in range(nb)], dma_key="vst")
        if kv_only:
            return
        S.barrier()
        self.carve_attn()
        S.add("sp", lambda e: e.dma_start(out=self.MASK, in_=self.cmask), writes=["mask"], dma_key="mask")
        S.add("sp", lambda e: e.dma_start(out=self.VALT[:, 0:nb, :], in_=self.valt.rearrange("p (b j) -> p b j", j=NKB)[:, b0:b0 + nb, :]), writes=["valt"], dma_key="valt")
        iters = [(hd, hd % 2, qb, (hd * nb + qb) % 2) for hd in range(8) for qb in range(nb)]

        def head_loads(hd):
            sl = hd % 2
            QTv = self.QTD.rearrange("(g h) p t -> h p g t", h=8)
            S.add("sp", lambda e: e.dma_start(out=self.QH[sl][:, :, 0:n], in_=QTv[hd][:, :, t0:t0 + n]), writes=[("qh", sl)], dma_key=("qh", sl))
            for g in range(3):
                wb = GW[g] // 128
                S.add("sp", lambda e, g=g: e.dma_start(out=self.KH[sl][g][:, 0:n + GW[g]], in_=self.KTD[g * 8 + hd][:, t0 - GW[g]:t0 + n]),
                      writes=[("kh", sl, g)], dma_key=("kh", sl, g))
                S.add("sp", lambda e, g=g, wb=wb: e.dma_start(out=self.VH[sl][g][:, 0:nb + wb, :], in_=self.VD[g * 8 + hd, b0 - wb:b0 + nb].rearrange("b p d -> p b d")),
                      writes=[("vh", sl, g)], dma_key=("vh", sl, g))
        head_loads(0)
        self._a_scores(*iters[0])
        self._a_soft(*iters[0])
        for i, itx in enumerate(iters):
            nxt = iters[i + 1] if i + 1 < len(iters) else None
            if nxt is not None:
                if nxt[2] == 0:
                    head_loads(nxt[0])
                self._a_scores(*nxt)
            self._b_tr(*itx)
            if nxt is not None:
                self._a_soft(*nxt)
            self._b_pv(*itx)
        S.barrier()
        self.carve_linear()
        Wo = self.attn_w_out[0].rearrange("(k p) c -> p k c", p=128)
        wk0, w0 = self.wload(Wo[:, 0:4, :], 4, D)
        wk1, w1 = self.wload(Wo[:, 4:8, :], 4, D)
        for si, (o, m) in enumerate(subs_of(n)):
            for oc in range(KC):
                bk, bank = self.bank(m)

                def mm(e, oc=oc, o=o, m=m, bank=bank):
                    for kc in range(8):
                        w = w0 if kc < 4 else w1
                        r = e.matmul(bank, lhsT=w[:, kc % 4, oc * 128:(oc + 1) * 128], rhs=self.GB[:, kc, o:o + m], start=(kc == 0), stop=(kc == 7))
                    return r
                S.add("pe", mm, reads=[wk0, wk1] + [("gb", kc, si) for kc in range(8)], writes=[bk])
                self.evac_addx(oc, si, o, m, bk, bank)

    def _attn_v(self, wk, w, ti, VST):
        S = self.S
        si = (ti * 128) // 512
        bk, bank = self.bank(512)

        def mm(e):
            for kc in range(KC):
                r = e.matmul(bank, lhsT=self.HB[:, kc, ti * 128:(ti + 1) * 128], rhs=w[:, kc, :], start=(kc == 0), stop=(kc == KC - 1))
            return r
        S.add("pe", mm, reads=[wk] + [("hb", kc, si) for kc in range(KC)], writes=[bk])
        if ti % 2 == 0:
            S.add("act", lambda e: e.activation(out=VST[:, ti, :], in_=bank, func=AF.Copy), reads=[bk], writes=[("vst", ti)])
        else:
            S.add("dve", lambda e: e.tensor_copy(out=VST[:, ti, :], in_=bank), reads=[bk], writes=[("vst", ti)])

    def _a_scores(self, hd, sl, qb, ss):
        S = self.S
        PS = self.PS
        QH, KH = self.QH[sl], self.KH[sl]
        q0 = qb * 128

        def scores(e):
            for g, kc0, ncol, pc0 in [(2, 0, 512, 0), (2, 512, 512, 512), (2, 1024, 512, 1024), (2, 1536, 512, 1536), (2, 2048, 128, 2048),
                                      (0, 0, 256, 2176), (1, 512, 128, 2432), (1, 0, 512, 2560)]:
                r = e.matmul(PS[:, pc0:pc0 + ncol], lhsT=QH[:, g, q0:q0 + 128], rhs=KH[g][:, q0 + kc0:q0 + kc0 + ncol], start=True, stop=True)
            return r
        S.add("pe", scores, reads=[("qh", sl)] + [("kh", sl, g) for g in range(3)], writes=[("ps", b) for b in range(6)])

    def _a_soft(self, hd, sl, qb, ss):
        S = self.S
        PS = self.PS
        SSb, PP = self.SS[0], self.PP[ss]
        S.add("dve", lambda e: e.tensor_tensor(out=SSb, in0=PS[:, 0:NSC], in1=self.MASK, op=ALU.add), reads=[("ps", b) for b in range(6)] + ["mask"], writes=[("ss", 0)])
        c = ss * 2
        mx = self.SMALL[:, c:c + 1]
        nbv = self.SMALL[:, c + 1:c + 2]
        S.add("dve", lambda e: e.reduce_max(out=mx, in_=SSb, axis=AX.X), reads=[("ss", 0)], writes=[("mx", ss)])
        S.add("dve", lambda e: e.tensor_scalar(out=nbv, in0=mx, scalar1=-SCALE, scalar2=None, op0=ALU.mult), reads=[("mx", ss)], writes=[("nb", ss)])
        S.add("act", lambda e: e.activation(out=PP, in_=SSb, func=AF.Exp, scale=SCALE, bias=nbv), reads=[("ss", 0), ("nb", ss)], writes=[("pp", ss)])

    def _b_tr(self, hd, sl, qb, ss):
        S = self.S
        PS = self.PS
        PP, PT = self.PP[ss], self.PT[ss]
        b6 = PS[:, 6 * 512:7 * 512].bitcast(BF16)
        for r_ in range(3):
            def tr(e, r_=r_):
                for x in range(8):
                    jx = r_ * 8 + x
                    r = e.transpose(b6[:, x * 128:(x + 1) * 128], PP[:, jx * 128:(jx + 1) * 128], self.IDB)
                return r
            S.add("pe", tr, reads=[("pp", ss), "idb"], writes=[("ps", 6)])
            vb = self.VALT[:, qb, r_ * 8:(r_ + 1) * 8].unsqueeze(2).to_broadcast([128, 8, 128])
            S.add("dve", lambda e, r_=r_, vb=vb: e.tensor_tensor(out=PT[:, r_ * 8:(r_ + 1) * 8, :], in0=b6.rearrange("p (j q) -> p j q", q=128), in1=vb, op=ALU.mult),
                  reads=[("ps", 6), "valt"], writes=[("pt", ss, r_)])

    def _b_pv(self, hd, sl, qb, ss):
        S = self.S
        PS = self.PS
        VH = self.VH[sl]
        q0 = qb * 128
        PT = self.PT[ss]
        O = PS[:, 7 * 512:7 * 512 + 128]
        DN = PS[:, 7 * 512 + 256:7 * 512 + 384]

        def pv(e):
            for jx, (g, kb) in enumerate(SC_ORDER):
                e.matmul(O, lhsT=VH[g][:, qb + kb, :], rhs=PT[:, jx, :], start=(jx == 0), stop=(jx == NKB - 1))
            for jx in range(NKB):
                r = e.matmul(DN, lhsT=self.ONESB, rhs=PT[:, jx, :], start=(jx == 0), stop=(jx == NKB - 1))
            return r
        S.add("pe", pv, reads=[("pt", ss, r_) for r_ in range(3)] + [("vh", sl, g) for g in range(3)] + ["onesb"], writes=[("ps", 7)])
        S.add("dve", lambda e: e.tensor_scalar(out=self.RD, in0=DN, scalar1=1e-30, scalar2=None, op0=ALU.add), reads=[("ps", 7)], writes=["rd"])
        S.add("dve", lambda e: e.reciprocal(out=self.RD, in_=self.RD), reads=["rd"], writes=["rd"])
        dst = self.GB[:, hd, q0:q0 + 128]
        si = q0 // 512
        S.add("dve", lambda e: e.tensor_tensor(out=dst, in0=O, in1=self.RD, op=ALU.mult), reads=[("ps", 7), "rd"], writes=[("gb", hd, si)])

    def build(self):
        S = self.S
        self.init()
        if self.nlayers > 1:
            self.sgu_init()
        L = self.nlayers
        last = L - 1
        for layer in range(L):
            kind = layer % 3
            j = layer // 3
            src = self.xT if layer == 0 else self.XR
            if layer == 0:
                tl = [(0, 1, True)] + [(b, n, False) for b, n in tiles_of(1, NB, 5)]
            elif layer == 1:
                tl = [(b, n, False) for b, n in tiles_of(1, NB, 5)]
            elif layer == 2:
                tl = [(b, n, True) for b, n in tiles_of(1, 17, 2)] + [(b, n, False) for b, n in tiles_of(17, NB, 3)]
            else:
                tl = [(17, 1, True)] + [(b, n, False) for b, n in tiles_of(18, NB, 2)]
            for (b0, nb, partial) in tl:
                t0, n = b0 * 128, nb * 128
                self.load_x(src, t0, n)
                if kind == 0:
                    fix = None
                    if b0 <= 18 < b0 + nb:
                        fix = (18 - b0) * 128
                    self.pool_mixer(j, layer, n, partial, fix)
                elif kind == 1:
                    self.sgu_mixer(layer, n)
                else:
                    self.attn_mixer(layer, b0, nb, partial)
                if partial:
                    continue
                self.mlp(layer, n)
                if layer == last and not self.debug_x:
                    self.norm(n, 128, inplace=True)
                    self.store_x(self.OUT, t0 - HALO, n)
                elif layer == last:
                    self.store_x(self.OUT, t0, n)
                else:
                    self.store_x(self.XR, t0, n)
        S.emit()
        return self.nc


def _feat(v):
    v = np.asarray(v, np.float32).reshape(-1, KC, 128)
    return np.ascontiguousarray(v.transpose(2, 0, 1)).reshape(128, -1)


def prep_shared(inp, nlayers=4):
    sh = {}
    g = np.concatenate([_feat(inp["norm_mix"]), _feat(inp["norm_mlp"]), _feat(inp["norm_final"]),
                        _feat(inp["pool_scale"]), _feat(inp["sgu_v_norm"])], axis=1)
    sh["gains"] = np.ascontiguousarray(g, np.float32)
    sh["idb"] = np.eye(128).astype(ml_dtypes.bfloat16)
    for k in ["pool_w_in", "pool_w_group", "pool_w_out", "mlp_w_up", "mlp_w_down"]:
        sh[k] = np.ascontiguousarray(inp[k], np.float32)
    if nlayers > 1:
        sh["sgu_w_in"] = np.ascontiguousarray(inp["sgu_w_in"], np.float32)
        sh["sgu_w_out"] = np.ascontiguousarray(inp["sgu_w_out"], np.float32)
        ws = np.asarray(inp["sgu_w_s"], np.float32)[0]
        sh["wsT"] = np.ascontiguousarray(ws.transpose(2, 0, 1)).reshape(128, 1024)
        bs = np.asarray(inp["sgu_b_s"], np.float32)[0]
        sh["bsb"] = np.ascontiguousarray(np.broadcast_to(bs.reshape(1, 1024), (128, 1024)))
    if nlayers > 2:
        sh["attn_w_qkv"] = np.ascontiguousarray(inp["attn_w_qkv"], np.float32)
        sh["attn_w_out"] = np.ascontiguousarray(inp["attn_w_out"], np.float32)
        cm = np.full((128, NSC), -1e30, np.float32)
        q = np.arange(128)[:, None]
        kk = np.arange(128)[None, :]
        for jx, (g_, kb) in enumerate(SC_ORDER):
            delta = q + GW[g_] - (kb * 128 + kk)
            ok = (delta >= 0) & (delta <= GW[g_]) & (delta % GD[g_] == 0)
            cm[:, jx * 128:(jx + 1) * 128] = np.where(ok, 0.0, -1e30)
        sh["cmask"] = cm
    return sh


def prep_core(inp, c, nlayers=4):
    pc = {}
    x = np.asarray(inp["x"], np.float32)[0]
    S0 = c * OWN
    xp = np.zeros((NT, D), np.float32)
    lo = S0 - HALO
    a = max(lo, 0)
    xp[a - lo:] = x[a:S0 + OWN]
    pc["xT"] = np.ascontiguousarray(xp.T).reshape(KC, 128, NT)
    icnt = np.zeros((4, 16), np.float32)
    for g_ in range(4):
        w = 2 << g_
        for t in range(16):
            icnt[g_, t] = 1.0 / min(t + 1, w) if c == 0 else 1.0 / w
    tril = (np.arange(128)[:, None] <= np.arange(128)[None, :]).astype(np.float32)
    pc["consts"] = np.ascontiguousarray(np.concatenate(
        [np.ones((128, 128), np.float32), np.broadcast_to(icnt.reshape(1, 64), (128, 64)), tril], axis=1))
    if nlayers > 2:
        valid_blk = np.array([1.0 if (lo + b * 128) >= 0 else 0.0 for b in range(NB)], np.float32)
        vt = np.zeros((NB, NKB), np.float32)
        for b in range(NB):
            for jx, (g_, kb) in enumerate(SC_ORDER):
                kblk = b - GW[g_] // 128 + kb
                vt[b, jx] = valid_blk[kblk] if kblk >= 0 else 0.0
        pc["valt"] = np.ascontiguousarray(np.broadcast_to(vt.reshape(1, -1), (128, NB * NKB)))
    return pc


def kernel(**inputs):
    b = Builder(4, False)
    nc = b.build()
    sh = prep_shared(inputs, 4)
    in_maps = []
    for c in range(NCORES):
        m = dict(sh)
        m.update(prep_core(inputs, c, 4))
        in_maps.append(m)
    res = run_bass_kernel_spmd(nc, in_maps, core_ids=list(range(NCORES)))
    outs = []
    for c in range(NCORES):
        o = np.asarray(res.results[c]["OUT"], np.float32).reshape(D, OWN)
        outs.append(o.T)
    return np.ascontiguousarray(np.concatenate(outs, axis=0)).reshape(1, NCORES * OWN, D).astype(np.float32)
```
